# Optimizing a Trainium2 kernel written in Bass

```python
import math
import jax, jax.numpy as jnp
from jax import lax
import numpy as np

D_MODEL = 1024
BATCH = 32
SEQ = 2048
DEPTH = 1

HEAD_DIM = 64
SB_HEADS = 8
DIL_GROUPS = ((128, 1), (512, 4), (2048, 16))
DIL_HEADS = 4
MEM_HEADS = 4
MEM_HEAD_DIM = 128
MEM_LEN = 256
N_BRANCHES = 3
D_FF = ((-(-8 * D_MODEL // 3) + 255) // 256) * 256
BLOCK = 128
ROPE_THETA = 10000.0
NORM_EPS = 1e-6
NEG_INF = -1e30

SB_W = SB_HEADS * HEAD_DIM
DIL_W = DIL_HEADS * HEAD_DIM
MEM_W = MEM_HEADS * MEM_HEAD_DIM
IN_SPLITS = (SB_W,) * 3 + (DIL_W,) * (3 * len(DIL_GROUPS)) + (MEM_W,)
D_IN = sum(IN_SPLITS)

kernel_name = 'hybrid_stickbreak_dilated_memory_block'


def rms_norm(x, g):
    xf = x.astype(jnp.float32)
    y = xf * lax.rsqrt(jnp.mean(xf * xf, axis=-1, keepdims=True) + NORM_EPS)
    return (y * g.astype(jnp.float32)).astype(x.dtype)


def rope(x, pos):
    dh = x.shape[-1]
    half = dh // 2
    inv_freq = ROPE_THETA ** (-jnp.arange(half, dtype=jnp.float32) * 2.0 / dh)
    ang = pos.astype(jnp.float32)[:, None] * inv_freq[None, :]
    cos = jnp.cos(ang)[None, :, None, :]
    sin = jnp.sin(ang)[None, :, None, :]
    xf = x.astype(jnp.float32)
    x1, x2 = xf[..., :half], xf[..., half:]
    return jnp.concatenate([x1 * cos - x2 * sin, x2 * cos + x1 * sin], axis=-1).astype(x.dtype)


def stick_breaking_attention(q, k, v):
    B, S, H, dh = q.shape
    scale = dh ** -0.5
    outs = []
    for i in range(S // BLOCK):
        t0 = i * BLOCK
        t1 = t0 + BLOCK
        z = jnp.einsum('bqhd,bkhd->bhqk', q[:, t0:t1], k[:, :t1]).astype(jnp.float32) * scale
        t_pos = t0 + jnp.arange(BLOCK)[:, None]
        s_pos = jnp.arange(t1)[None, :]
        causal = s_pos < t_pos
        log_beta = jax.nn.log_sigmoid(z)
        log_keep = jnp.where(causal, jax.nn.log_sigmoid(-z), 0.0)
        log_keep_after = lax.cumsum(log_keep, axis=3, reverse=True) - log_keep
        weight = jnp.where(causal, jnp.exp(log_beta + log_keep_after), 0.0)
        outs.append(jnp.einsum('bhqk,bkhd->bqhd', weight, v[:, :t1].astype(jnp.float32)))
    return jnp.concatenate(outs, axis=1).astype(q.dtype)


def banded_attention(q, k, v, span):
    N, L, H, dh = q.shape
    nb = -(-L // BLOCK)
    pad = nb * BLOCK - L
    padf = lambda t: jnp.pad(t, ((0, 0), (0, pad), (0, 0), (0, 0))).reshape(N, nb, BLOCK, H, dh)
    qb, kb, vb = padf(q), padf(k), padf(v)
    prev = lambda t: jnp.concatenate([jnp.zeros_like(t[:, :1]), t[:, :-1]], axis=1)
    kk = jnp.concatenate([prev(kb), kb], axis=2)
    vv = jnp.concatenate([prev(vb), vb], axis=2)
    s = jnp.einsum('nbqhd,nbkhd->nbhqk', qb, kk).astype(jnp.float32) * (dh ** -0.5)
    qi = jnp.arange(BLOCK)[:, None] + BLOCK
    kj = jnp.arange(2 * BLOCK)[None, :]
    dist = qi - kj
    band = (dist >= 0) & (dist <= span)
    has_prev = (jnp.arange(nb)[:, None, None] > 0) | (kj[None] >= BLOCK)
    valid = band[None] & has_prev
    s = jnp.where(valid[None, :, None], s, NEG_INF)
    m = jnp.max(s, axis=-1, keepdims=True)
    p = jnp.exp(s - m)
    den = jnp.sum(p, axis=-1, keepdims=True)
    o = jnp.einsum('nbhqk,nbkhd->nbhqd', p, vv.astype(jnp.float32)) / den
    lse = (m + jnp.log(den))[..., 0]
    o = o.transpose(0, 1, 3, 2, 4).reshape(N, nb * BLOCK, H, dh)[:, :L]
    lse = lse.transpose(0, 1, 3, 2).reshape(N, nb * BLOCK, H)[:, :L]
    return o, lse


def dilated_window_attention(q, k, v, window, dilation):
    B, S, H, dh = q.shape
    L = S // dilation
    def to_sub(t):
        return t.reshape(B, L, dilation, H, dh).transpose(0, 2, 1, 3, 4).reshape(B * dilation, L, H, dh)
    o, lse = banded_attention(to_sub(q), to_sub(k), to_sub(v), window // dilation)
    o = o.reshape(B, dilation, L, H, dh).transpose(0, 2, 1, 3, 4).reshape(B, S, H, dh)
    lse = lse.reshape(B, dilation, L, H).transpose(0, 2, 1, 3).reshape(B, S, H)
    return o, lse


def memory_cross_attention(q, k, v):
    s = jnp.einsum('bshd,bmhd->bhsm', q, k).astype(jnp.float32) * (q.shape[-1] ** -0.5)
    p = jax.nn.softmax(s, axis=-1)
    return jnp.einsum('bhsm,bmhd->bshd', p, v.astype(jnp.float32)).astype(q.dtype)


def setup_inputs(seed: int = 0) -> dict:
    key = jax.random.key(seed)
    ks = jax.random.split(key, 20)
    def w(k, shape, fan_in):
        return jax.random.normal(k, shape, jnp.float32) * (fan_in ** -0.5)
    def gain(k):
        return 1.0 + 0.05 * jax.random.normal(k, (DEPTH, D_MODEL), jnp.float32)
    return {
        'x': jax.random.normal(ks[0], (BATCH, SEQ, D_MODEL), jnp.float32),
        'mem': jax.random.normal(ks[1], (BATCH, MEM_LEN, D_MODEL), jnp.float32),
        'g_pre_mix': gain(ks[2]),
        'g_post_mix': gain(ks[3]),
        'g_pre_ffn': gain(ks[4]),
        'g_post_ffn': gain(ks[5]),
        'g_mem': gain(ks[6]),
        'w_in': w(ks[7], (DEPTH, D_MODEL, D_IN), D_MODEL),
        'w_mem_kv': w(ks[8], (DEPTH, D_MODEL, 2 * MEM_W), D_MODEL),
        'w_br_sb': w(ks[9], (DEPTH, SB_W, D_MODEL), SB_W),
        'w_br_dil': w(ks[10], (DEPTH, DIL_W, D_MODEL), DIL_W),
        'w_br_mem': w(ks[11], (DEPTH, MEM_W, D_MODEL), MEM_W),
        'w_gate': w(ks[12], (DEPTH, D_MODEL, N_BRANCHES * D_MODEL), D_MODEL),
        'b_gate': 0.02 * jax.random.normal(ks[13], (DEPTH, N_BRANCHES * D_MODEL), jnp.float32),
        'w_o': w(ks[14], (DEPTH, D_MODEL, D_MODEL), D_MODEL),
        'w_ffn_in': w(ks[15], (DEPTH, D_MODEL, 2 * D_FF), D_MODEL),
        'w_ffn_out': w(ks[16], (DEPTH, D_FF, D_MODEL), D_FF),
    }


def reference(x, mem, g_pre_mix, g_post_mix, g_pre_ffn, g_post_ffn, g_mem, w_in, w_mem_kv,
              w_br_sb, w_br_dil, w_br_mem, w_gate, b_gate, w_o, w_ffn_in, w_ffn_out):
    B, S, D = x.shape
    pos = jnp.arange(S)
    split_idx = [int(i) for i in np.cumsum(IN_SPLITS)[:-1]]
    n_g = len(DIL_GROUPS)
    for l in range(DEPTH):
        h = rms_norm(x, g_pre_mix[l])
        proj = jnp.einsum('bsd,de->bse', h, w_in[l])
        parts = jnp.split(proj, split_idx, axis=-1)
        heads = lambda t, n, dh: t.reshape(B, S, n, dh)

        q_a, k_a, v_a = (heads(t, SB_HEADS, HEAD_DIM) for t in parts[0:3])
        o_a = stick_breaking_attention(q_a, k_a, v_a).reshape(B, S, SB_W)

        outs, lses = [], []
        for g, (window, dilation) in enumerate(DIL_GROUPS):
            q_g, k_g, v_g = (heads(t, DIL_HEADS, HEAD_DIM) for t in parts[3 + 3 * g: 6 + 3 * g])
            o_g, lse_g = dilated_window_attention(rope(q_g, pos), rope(k_g, pos), v_g, window, dilation)
            outs.append(o_g)
            lses.append(lse_g)
        alpha = jax.nn.softmax(jnp.stack(lses, axis=0), axis=0)[..., None]
        o_b = jnp.sum(alpha * jnp.stack(outs, axis=0), axis=0).astype(x.dtype).reshape(B, S, DIL_W)

        q_c = heads(parts[3 + 3 * n_g], MEM_HEADS, MEM_HEAD_DIM)
        kv_m = jnp.einsum('bmd,de->bme', rms_norm(mem, g_mem[l]), w_mem_kv[l])
        k_m = kv_m[..., :MEM_W].reshape(B, MEM_LEN, MEM_HEADS, MEM_HEAD_DIM)
        v_m = kv_m[..., MEM_W:].reshape(B, MEM_LEN, MEM_HEADS, MEM_HEAD_DIM)
        o_c = memory_cross_attention(q_c, k_m, v_m).reshape(B, S, MEM_W)

        y_a = jnp.einsum('bse,ed->bsd', o_a, w_br_sb[l])
        y_b = jnp.einsum('bse,ed->bsd', o_b, w_br_dil[l])
        y_c = jnp.einsum('bse,ed->bsd', o_c, w_br_mem[l])
        gates = jax.nn.sigmoid(jnp.einsum('bsd,de->bse', h, w_gate[l]) + b_gate[l]).reshape(B, S, N_BRANCHES, D)
        merged = gates[:, :, 0] * y_a + gates[:, :, 1] * y_b + gates[:, :, 2] * y_c
        mix = jnp.einsum('bsd,de->bse', merged, w_o[l])
        x = x + rms_norm(mix, g_post_mix[l])

        h2 = rms_norm(x, g_pre_ffn[l])
        gu = jnp.einsum('bsd,df->bsf', h2, w_ffn_in[l])
        f = jax.nn.silu(gu[..., :D_FF]) * gu[..., D_FF:]
        f = jnp.einsum('bsf,fd->bsd', f, w_ffn_out[l])
        x = x + rms_norm(f, g_post_ffn[l])
    return x
```

```python
import contextlib
import numpy as np
import concourse.bass as bass
import concourse.mybir as mybir
from concourse.bass_utils import run_bass_kernel_spmd

F32 = mybir.dt.float32
BF16 = mybir.dt.bfloat16
AF = mybir.ActivationFunctionType
ALU = mybir.AluOpType

SEQ = 2048
D = 1024
MEM = 256
NCORES = 8
SEQ_THREADS = False
NTB = SEQ // 128
EPS = 1e-6
D_FF = 2816
NJ = D_FF // 128

T8 = 4096
OFF_IN = 0
OFF_MEM = 12 * T8
OFF_M = 14 * T8
MT = 8 * 384 + 10 * 128
OFF_WO = OFF_M + 8 * MT
OFF_FI = OFF_WO + 2 * T8
OFF_FO = OFF_FI + 11 * T8
TOT = OFF_FO + NJ * 1024
assert TOT == 167936 and TOT % T8 == 0

C_COS = 0
C_SIN = 2048
C_GPM = 4096
C_GPF = 5120
C_GTPM = 6144
C_GTPF = 6152
C_GTMEM = 6160
C_BG = 6168
NCF = 6192
NCB = 1536


class Sched:
    ENGS = ("pe", "act", "dve", "pool", "sp")
    EPOCH = 20000

    def __init__(self):
        self.ops = []
        self.lastw = {}
        self.readers = {}
        self.dma_cnt = {}
        self.phase = "init"
        self.thread = None
        self.fences = {}
        self.tlast = {}
        self.dry = False

    def fence(self, t):
        if self.dry:
            return
        self.fences[t] = set(self.tlast.get(t, {}).values())

    def add(self, eng, fn, r=(), w=(), dma=None):
        if self.dry:
            return None
        i = len(self.ops)
        deps = set()
        if self.thread is not None:
            deps |= self.fences.get(self.thread, set())
            self.tlast.setdefault(self.thread, {})[(eng, dma)] = i
        for x in r:
            lw = self.lastw.get(x)
            if lw is not None:
                deps.add(lw)
        for x in w:
            lw = self.lastw.get(x)
            if lw is not None:
                deps.add(lw)
            for ri in self.readers.get(x, {}).values():
                deps.add(ri)
        op = dict(eng=eng, fn=fn, deps=deps, dma=dma, sig=None, need=False, phase=self.phase)
        if dma is not None:
            c = self.dma_cnt.get(dma, 0) + 1
            self.dma_cnt[dma] = c
            op["sig"] = ("dma", dma, 16 * c)
            op["need"] = True
        self.ops.append(op)
        for x in w:
            self.lastw[x] = i
            self.readers[x] = {}
        for x in r:
            self.readers.setdefault(x, {})[(eng, dma)] = i
        return i

    def barrier(self):
        last = {}
        for i, op in enumerate(self.ops):
            if op["dma"] is not None and op["dma"].startswith("cv"):
                continue
            last[(op["eng"], op["dma"])] = i
        for e in self.ENGS:
            deps = set(v for k, v in last.items() if not (k[0] == e and k[1] is None))
            self.ops.append(dict(eng=e, fn=None, deps=deps, dma=None, sig=None, need=False))
        self.lastw = {k: v for k, v in self.lastw.items() if isinstance(k, tuple) and k and k[0] == "wbf"}
        self.readers = {}
        self.fences = {}
        self.tlast = {}

    def finalize(self):
        ops = self.ops
        for op in ops:
            for d in op["deps"]:
                dop = ops[d]
                if dop["dma"] is not None:
                    continue
                if dop["eng"] != op["eng"] or dop["eng"] != "pe":
                    dop["need"] = True
        cnt = {e: 0 for e in self.ENGS}
        for op in ops:
            if op["dma"] is None and op["need"] and op["fn"] is not None:
                e = op["eng"]
                cnt[e] += 1
                op["sig"] = ("eng", e, (cnt[e] - 1) // self.EPOCH, (cnt[e] - 1) % self.EPOCH + 1)
        self.nepoch = {e: max(1, (cnt[e] + self.EPOCH - 1) // self.EPOCH) for e in self.ENGS}

    def emit(self, eng_name, e, semtab, nc=None):
        ops = self.ops
        waited = {}
        cur_scope = None
        cur_phase = None
        for op in ops:
            if op["eng"] != eng_name:
                continue
            if nc is not None and op["fn"] is not None and op["phase"] != cur_phase:
                if cur_scope is not None:
                    cur_scope.__exit__(None, None, None)
                cur_phase = op["phase"]
                cur_scope = nc.named_scope(cur_phase)
                cur_scope.__enter__()
            for d in sorted(op["deps"]):
                sig = ops[d]["sig"]
                if sig is None:
                    continue
                if sig[0] == "eng":
                    if sig[1] == eng_name and eng_name == "pe":
                        continue
                    key = ("eng", sig[1], sig[2])
                    val = sig[3]
                else:
                    key = ("dma", sig[1])
                    val = sig[2]
                if waited.get(key, 0) >= val:
                    continue
                waited[key] = val
                e.wait_ge(semtab[key], val)
            if op["fn"] is None:
                continue
            ins = op["fn"](e)
            sig = op["sig"]
            if sig is not None:
                if sig[0] == "eng":
                    ins.then_inc(semtab[("eng", sig[1], sig[2])], 1)
                else:
                    ins.then_inc(semtab[("dma", sig[1])], 16)
        if cur_scope is not None:
            cur_scope.__exit__(None, None, None)
        if eng_name == "sp":
            for name, c in self.dma_cnt.items():
                key = ("dma", name)
                if waited.get(key, 0) < 16 * c:
                    e.wait_ge(semtab[key], 16 * c)


def build_program(nseq, dbg=False, scopes=False):
    nc = bass.Bass("TRN2", target_bir_lowering=False)
    x_d = nc.dram_tensor("x", [nseq * SEQ, D], F32, kind="ExternalInput").ap()
    mem_d = nc.dram_tensor("mem", [nseq * MEM, D], F32, kind="ExternalInput").ap()
    wall_d = nc.dram_tensor("wall", [128, TOT], F32, kind="ExternalInput").ap()
    cf_d = nc.dram_tensor("cf", [128, NCF + NCB], F32, kind="ExternalInput").ap()
    out_d = nc.dram_tensor("out", [nseq * SEQ, D], F32, kind="ExternalOutput").ap()
    wbf_d = nc.dram_tensor("wallbf", [128, TOT], BF16).ap()
    dbg_d = {}
    if dbg:
        for nm, n in (("oa", 8192), ("ob", 4096), ("oc", 8192), ("mg", 16384), ("ht", 16384), ("xq", 2048), ("xk", 2048), ("xv", 2048)):
            dbg_d[nm] = nc.dram_tensor("dbg_" + nm, [128, n], BF16, kind="ExternalOutput").ap()

    S = Sched()
    es = contextlib.ExitStack()
    with es:
        cf = es.enter_context(nc.sbuf_tensor("cf32", [128, NCF], F32))
        cb = es.enter_context(nc.sbuf_tensor("cb16", [128, 1920], BF16))
        hT_t = es.enter_context(nc.sbuf_tensor("hT", [128, 8 * SEQ], BF16))
        AR = 65536
        arena = es.enter_context(nc.sbuf_tensor("arena", [128, AR], BF16))
        ps = [es.enter_context(nc.psum_tensor(f"ps{i}", [128, 512], F32)) for i in range(8)]

        def A16(off, n):
            return arena[:, off:off + n]

        def A32(off, n):
            return arena[:, off:off + 2 * n].bitcast(F32)

        hT = hT_t[:, :].rearrange("p (k t) -> p k t", k=8)
        ident = cb[:, 0:128]
        negmaskA = cb[:, 128:256]
        negTri = cb[:, 256:384]
        negmaskB = cb[:, 384:640]
        negmask2 = cb[:, 384:896]
        perm = cb[:, 896:1024]
        band01 = cb[:, 1024:1536]
        ones = cb[:, 1536:1664]
        negones = cb[:, 1664:1792]
        zeros = cb[:, 1792:1920]
        cosT = cf[:, C_COS:C_COS + 2048]
        sinT = cf[:, C_SIN:C_SIN + 2048]
        gpm = cf[:, C_GPM:C_GPM + 1024]
        gpf = cf[:, C_GPF:C_GPF + 1024]

        def MM(out, lhsT, rhs, start, stop, r, w, **kw):
            S.add("pe", lambda e: e.matmul(out, lhsT=lhsT, rhs=rhs, start=start, stop=stop, **kw), r=r, w=w)

        def TR(out, in_, r, w):
            S.add("pe", lambda e: e.transpose(out, in_, ident), r=r, w=w)

        def ACT(out, in_, func, r, w, **kw):
            S.add("act", lambda e: e.activation(out=out, in_=in_, func=func, **kw), r=r, w=w)

        def DMA(out, in_, r, w, name, q="sp"):
            S.add(q, lambda e: e.dma_start(out=out, in_=in_), r=r, w=w, dma=name)

        def TT(eng, out, in0, in1, op, r, w):
            S.add(eng, lambda e: e.tensor_tensor(out=out, in0=in0, in1=in1, op=op), r=r, w=w)

        def TS(eng, out, in0, s1, op0, r, w, s2=None, op1=None):
            if op1 is None:
                S.add(eng, lambda e: e.tensor_scalar(out=out, in0=in0, scalar1=s1, scalar2=None, op0=op0), r=r, w=w)
            else:
                S.add(eng, lambda e: e.tensor_scalar(out=out, in0=in0, scalar1=s1, scalar2=s2, op0=op0, op1=op1), r=r, w=w)

        def STT(eng, out, in0, scalar, in1, op0, op1, r, w):
            S.add(eng, lambda e: e.scalar_tensor_tensor(out=out, in0=in0, scalar=scalar, in1=in1, op0=op0, op1=op1), r=r, w=w)

        def CP(eng, out, in_, r, w):
            if eng == "act":
                ACT(out, in_, AF.Copy, r, w)
            else:
                S.add(eng, lambda e: e.tensor_copy(out=out, in_=in_), r=r, w=w)

        def MEMSET(eng, ap, val, w):
            S.add(eng, lambda e: e.memset(ap, val), w=w)

        def RECIP(out, in_, r, w):
            S.add("dve", lambda e: e.reciprocal(out=out, in_=in_), r=r, w=w)

        class Bank:
            def __init__(self):
                self.i = 0

            def next(self):
                k = self.i % 8
                self.i += 1
                return k

        bank = Bank()

        def PS(k):
            return ("ps", k)

        class WRing:
            def __init__(self, offs, size):
                self.offs = offs
                self.size = size
                self.i = 0

            def load(self, off, n):
                k = self.i % len(self.offs)
                self.i += 1
                v = A16(self.offs[k], n)
                DMA(v, wbf_d[:, off:off + n], r=wbf_res(off, n), w=[("ws", k)], name=f"w{k}")
                return v, ("ws", k)

        def rstd_from_ssq(ssq, tmp, rstd, r, w):
            ACT(tmp, ssq, AF.Ln, r=r, w=[w + ("ln",)], scale=1.0 / D, bias=EPS)
            ACT(rstd, tmp, AF.Exp, r=[w + ("ln",)], w=[w], scale=-0.5)

        S.phase = "P0"
        DMA(cf[:, :], cf_d[:, 0:NCF], r=[], w=["cf"], name="c0")
        cst32 = A32(0, NCB)
        DMA(cst32, cf_d[:, NCF:NCF + NCB], r=[], w=["cst32"], name="c0")
        CP("dve", cb[:, 0:NCB], cst32, r=["cst32"], w=["cb"])
        MEMSET("pool", ones, 1.0, w=["cb1"])
        MEMSET("pool", negones, -1.0, w=["cb2"])
        MEMSET("pool", zeros, 0.0, w=["cb3"])
        S.barrier()
        def conv(chunks, shared=False):
            idxs = []
            for n_, i in enumerate(chunks):
                nm = f"cvs{n_ % 3}" if shared else f"cv{i}"
                idxs.append(S.add("pool", lambda e, i=i: e.dma_start(out=wbf_d[:, i * T8:(i + 1) * T8], in_=wall_d[:, i * T8:(i + 1) * T8]),
                                  r=[], w=[("wbf", i)], dma=nm))
            if shared:
                for k in idxs:
                    op = S.ops[k]
                    op["sig"] = ("dma", op["dma"], 16 * S.dma_cnt[op["dma"]])

        def wbf_res(off, n):
            return [("wbf", c_) for c_ in range(off // T8, (off + n - 1) // T8 + 1)]

        conv_order = [12, 13, 11, 0, 1, 2, 3, 4, 9, 5, 6, 7, 8, 10] + list(range(14, TOT // T8))
        conv(conv_order[:14])

        OCT = A16(0, 8192).rearrange("p (k t) -> p k t", k=4)
        OBT = A16(8192, 4096).rearrange("p (k t) -> p k t", k=2)
        OAT = A16(12288, 8192).rearrange("p (k t) -> p k t", k=4)

        def norm_transpose(src_rows, nblk, dstT, gT_off, base, small, dname):
            xt = [A32(base + i * 2048, 1024) for i in range(4)]
            hn = [A16(base + 8192, 1024), A16(base + 9216, 1024)]
            junk = A16(base + 10240, 1024)
            gT = cf[:, gT_off:gT_off + 8].unsqueeze(2).broadcast_to([128, 8, 128])
            MEMSET("pool", small[:, 0:16], 0.0, w=[("ssq", t_) for t_ in range(nblk)])
            pst = {}

            def nt_a(tb):
                b = tb % 4
                DMA(xt[b], src_rows(tb), r=[], w=[("xt", b)], name=f"{dname}{b}")
                ACT(junk, xt[b], AF.Square, r=[("xt", b)], w=["junk", ("ssq", tb)], accum_out=small[:, tb:tb + 1])
                rstd_from_ssq(small[:, tb:tb + 1], small[:, 16 + tb:17 + tb], small[:, 32 + tb:33 + tb],
                              r=[("ssq", tb)], w=("rstd", tb))

            def nt_b(tb):
                b = tb % 2
                TS("dve", hn[b], xt[tb % 4], small[:, 32 + tb:33 + tb], ALU.mult, r=[("xt", tb % 4), ("rstd", tb)], w=[("hn", b)])
                k = bank.next()
                pst[tb] = k
                psT = ps[k][:, :].bitcast(BF16)
                for kc in range(8):
                    TR(psT[:, kc * 128:(kc + 1) * 128], hn[b][:, kc * 128:(kc + 1) * 128], r=[("hn", b)], w=[PS(k)])

            def nt_c(tb):
                k = pst[tb]
                psT = ps[k][:, :].bitcast(BF16)
                TT("dve", dstT[:, :, tb * 128:(tb + 1) * 128], psT.rearrange("p (k t) -> p k t", k=8), gT, ALU.mult,
                   r=[PS(k)], w=[("dstT", tb)])

            for t in range(nblk + 2):
                if 0 <= t - 2 < nblk:
                    nt_c(t - 2)
                if 0 <= t - 1 < nblk:
                    nt_b(t - 1)
                if t < nblk:
                    nt_a(t)

        def evac(i, out, in_, r, w):
            CP("act" if i % 2 == 0 else "dve", out, in_, r, w)

        for s in range(nseq):
            S.phase = f"s{s}_P1"
            L0 = 20480
            KMT = A16(37888, 1024).rearrange("p (k t) -> p k t", k=4)
            VM = A16(38912, 1024).rearrange("p (k t) -> p k t", k=2)
            MEMT = A16(56320, 2048).rearrange("p (k t) -> p k t", k=8)
            small = A32(L0 + 11264, 64)
            ring = WRing([41984, 41984 + MT], MT)
            wk, rk = ring.load(OFF_MEM, T8)
            wv, rv = ring.load(OFF_MEM + T8, T8)
            norm_transpose(lambda tb: mem_d[s * MEM + tb * 128: s * MEM + (tb + 1) * 128, :], 2, MEMT, C_GTMEM, L0, small, "x")
            norm_transpose(lambda tb: x_d[s * SEQ + tb * 128: s * SEQ + (tb + 1) * 128, :], NTB, hT, C_GTPM, L0, small, "x")
            for hc in range(4):
                k = bank.next()
                for kc in range(8):
                    MM(ps[k][:, 0:256], wk[:, kc * 512 + hc * 128: kc * 512 + hc * 128 + 128], MEMT[:, kc, :], kc == 0, kc == 7,
                       r=[rk, ("dstT", 0), ("dstT", 1)], w=[PS(k)])
                evac(hc, KMT[:, hc, :], ps[k][:, 0:256], r=[PS(k)], w=[("kmt", hc)])
            for mb in range(2):
                k = bank.next()
                for kc in range(8):
                    MM(ps[k][:, :], MEMT[:, kc, mb * 128:(mb + 1) * 128], wv[:, kc * 512:(kc + 1) * 512], kc == 0, kc == 7,
                       r=[rv, ("dstT", 0), ("dstT", 1)], w=[PS(k)])
                evac(mb, VM[:, mb, :], ps[k][:, :], r=[PS(k)], w=[("vm", mb)])
            S.barrier()
            if dbg and s == 0:
                DMA(dbg_d["ht"], hT_t[:, :], r=[], w=[], name="dbg")

            if s == 0:
                conv(conv_order[14:], shared=True)
            S.phase = f"s{s}_XY"
            xbank = Bank()
            ybank = Bank()

            def XB():
                k = xbank.i % 4
                xbank.i += 1
                return k

            def YB():
                k = 5 + ybank.i % 3
                ybank.i += 1
                return k

            def thread_Y():
                QCT = A16(39936, 8192).rearrange("p (k t) -> p k t", k=4)
                wq = A16(48128, T8)
                PB = [A16(52224 + i * 512, 512) for i in range(4)]
                RD = [A32(54272 + i * 1024, 512) for i in range(2)]
                DMA(wq, wbf_d[:, OFF_IN + 11 * T8: OFF_IN + 12 * T8], r=wbf_res(OFF_IN + 11 * T8, T8), w=["Ywq"], name="w2")
                for fc in range(4):
                    for tc in range(4):
                        k = YB()
                        for kc in range(8):
                            MM(ps[k][:, :], wq[:, kc * 512 + fc * 128: kc * 512 + fc * 128 + 128], hT[:, kc, tc * 512:(tc + 1) * 512],
                               kc == 0, kc == 7, r=["Ywq"], w=[PS(k)])
                        CP("dve", QCT[:, fc, tc * 512:(tc + 1) * 512], ps[k][:, :], r=[PS(k)], w=[("Yqct", fc, tc)])
                        yield
                cscale = float(128 ** -0.5)
                u = 0
                for h in range(4):
                    for tc in range(4):
                        k0, k1 = YB(), YB()
                        for mb, k in ((0, k0), (1, k1)):
                            MM(ps[k][:, :], KMT[:, h, mb * 128:(mb + 1) * 128], QCT[:, h, tc * 512:(tc + 1) * 512], True, True,
                               r=[("Yqct", h, tc)], w=[PS(k)])
                        pi0, pi1 = (2 * u) % 4, (2 * u + 1) % 4
                        ACT(PB[pi0], ps[k0][:, :], AF.Exp, r=[PS(k0)], w=[("Ypb", pi0)], scale=cscale)
                        ACT(PB[pi1], ps[k1][:, :], AF.Exp, r=[PS(k1)], w=[("Ypb", pi1)], scale=cscale)
                        yield
                        ko, kd = YB(), YB()
                        for mb, pi in ((0, pi0), (1, pi1)):
                            MM(ps[ko][:, :], VM[:, mb, h * 128:(h + 1) * 128], PB[pi], mb == 0, mb == 1, r=[("Ypb", pi)], w=[PS(ko)])
                        for mb, pi in ((0, pi0), (1, pi1)):
                            MM(ps[kd][:, :], ones, PB[pi], mb == 0, mb == 1, r=[("Ypb", pi)], w=[PS(kd)])
                        rd = RD[u % 2]
                        RECIP(rd, ps[kd][:, :], r=[PS(kd)], w=[("Yrd", u % 2)])
                        TT("dve", OCT[:, h, tc * 512:(tc + 1) * 512], ps[ko][:, :], rd, ALU.mult, r=[PS(ko), ("Yrd", u % 2)], w=[("Yoct", h, tc)])
                        u += 1
                        yield
                S.fence("Y")
                QZH = A16(37888, 4096).rearrange("p (e t) -> p e t", e=2)
                MEMSET("pool", QZH[64:128, 0, :], 0.0, w=["Yqz0"])
                MEMSET("pool", QZH[0:64, 1, :], 0.0, w=["Yqz1"])
                KTH = A16(41984, 2048)
                VGH = A16(44032, 2048).rearrange("p (k t) -> p k t", k=16)
                NUM = A32(46080, 2048)
                DEN = A32(50176, 2048)
                WQ = [A16(54272, 1024), A16(55296, 1024)]
                XB16 = [A16(56320, 512), A16(56832, 512)]
                WV = A16(58368, 1024)
                T1 = [A32(59392 + i * 1024, 512) for i in range(2)]
                T2 = [A32(61440 + i * 1024, 512) for i in range(2)]
                PBB = [A16(63488 + i * 512, 512) for i in range(3)]
                nrope = 0
                npb = 0
                for hp in range(2):
                    for g, dil in enumerate((1, 4, 16)):
                        nb = NTB // dil
                        for which in range(2):
                            src = wbf_d[:, OFF_IN + (3 + 2 * g + which) * T8: OFF_IN + (4 + 2 * g + which) * T8]
                            src = src.rearrange("p (k c) -> p k c", k=8)[:, :, hp * 128:(hp + 1) * 128]
                            DMA(WQ[which].rearrange("p (k c) -> p k c", k=8), src, r=wbf_res(OFF_IN + (3 + 2 * g + which) * T8, T8), w=[("Ywq", which)], name=f"w{3 + which}")
                        vt_off = OFF_IN + (9 if g < 2 else 10) * T8
                        voff = (256 if g == 1 else 0) + hp * 128
                        srcv = wbf_d[:, vt_off:vt_off + T8].rearrange("p (k c) -> p k c", k=8)[:, :, voff:voff + 128]
                        DMA(WV.rearrange("p (k c) -> p k c", k=8), srcv, r=wbf_res(vt_off, T8), w=["Ywv"], name="w5")
                        rtiles = [(which, tc) for which in range(2) for tc in range(4)]
                        rst = {}

                        def r_s1(i):
                            nonlocal nrope
                            which, tc = rtiles[i]
                            wt = WQ[which]
                            tsl = slice(tc * 512, (tc + 1) * 512)
                            k1 = YB()
                            for kc in range(8):
                                MM(ps[k1][:, :], wt[:, kc * 128:(kc + 1) * 128], hT[:, kc, tsl], kc == 0, kc == 7, r=[("Ywq", which)], w=[PS(k1)])
                            b = nrope % 2
                            nrope += 1
                            rst[i] = b
                            CP("dve", XB16[b], ps[k1][:, :], r=[PS(k1)], w=[("Yxb", b)])
                            TT("dve", T1[b], ps[k1][:, :], cosT[:, tsl], ALU.mult, r=[PS(k1)], w=[("Yt1", b)])

                        def r_s2(i):
                            which, tc = rtiles[i]
                            tsl = slice(tc * 512, (tc + 1) * 512)
                            b = rst[i]
                            k2 = YB()
                            MM(ps[k2][:, :], perm, XB16[b], True, True, r=[("Yxb", b)], w=[PS(k2)])
                            TT("dve", T2[b], ps[k2][:, :], sinT[:, tsl], ALU.mult, r=[PS(k2)], w=[("Yt2", b)])
                            if which == 0:
                                TT("dve", QZH[0:64, 0, tsl], T1[b][0:64, :], T2[b][0:64, :], ALU.add, r=[("Yt1", b), ("Yt2", b), "Yqz0"],
                                   w=[("Yqk", which, tc)])
                                TT("dve", QZH[64:128, 1, tsl], T1[b][64:128, :], T2[b][64:128, :], ALU.add, r=[("Yt1", b), ("Yt2", b), "Yqz1"],
                                   w=[("Yqk2", tc)])
                            else:
                                TT("dve", KTH[:, tsl], T1[b], T2[b], ALU.add, r=[("Yt1", b), ("Yt2", b)], w=[("Yqk", which, tc)])

                        for i in range(len(rtiles) + 1):
                            if i < len(rtiles):
                                r_s1(i)
                            if i >= 1:
                                r_s2(i - 1)
                            yield
                        for bb in range(4):
                            k = YB()
                            for j4 in range(4):
                                blk = bb * 4 + j4
                                r_, b_ = blk // nb, blk % nb
                                t0 = b_ * 128 * dil + r_
                                for kc in range(8):
                                    MM(ps[k][:, j4 * 128:(j4 + 1) * 128], hT[:, kc, t0:t0 + 127 * dil + 1:dil], WV[:, kc * 128:(kc + 1) * 128],
                                       kc == 0, kc == 7, r=["Ywv"], w=[PS(k)])
                            CP("dve", VGH[:, 4 * bb:4 * bb + 4, :], ps[k][:, :].rearrange("p (a c) -> p a c", a=4), r=[PS(k)], w=[("Yvg", bb)])
                            yield
                        qk_all = [("Yqk", wh, tc) for wh in range(2) for tc in range(4)] + [("Yqk2", tc) for tc in range(4)]
                        vg_all = [("Yvg", bb) for bb in range(4)]

                        def tok(r_, b_):
                            t0 = b_ * 128 * dil + r_
                            return slice(t0, t0 + 127 * dil + 1, dil)

                        batches = []
                        if g < 2:
                            for r_ in range(dil):
                                for b0 in range(0, nb, 2):
                                    batches.append([(r_, b0 + i) for i in range(2)])
                        else:
                            for r0 in range(0, 16, 2):
                                batches.append([(r0 + i, 0) for i in range(2)])
                        KO_B = 7
                        flat = [(bi, slot, r_, qb) for bi, blks in enumerate(batches) for slot, (r_, qb) in enumerate(blks)]
                        sst = {}

                        def b_s1(i):
                            nonlocal npb
                            bi, slot, r_, qb = flat[i]
                            ks = 5 + i % 2
                            has_prev = qb > 0
                            qz = QZH[:, :, tok(r_, qb)]
                            if has_prev:
                                MM(ps[ks][:, 0:256], KTH[:, tok(r_, qb - 1)], qz, True, True, r=qk_all, w=[PS(ks)])
                            MM(ps[ks][:, 256:512], KTH[:, tok(r_, qb)], qz, True, True, r=qk_all, w=[PS(ks)])
                            pi = npb % 3
                            npb += 1
                            sst[i] = pi
                            pb = PBB[pi]
                            lo_ = 0 if has_prev else 256
                            ACT(pb[:, lo_:512], ps[ks][:, lo_:512], AF.Exp, r=[PS(ks)], w=[("Ypbb", pi)], scale=0.125)
                            TT("dve", pb[:, lo_:512], pb[:, lo_:512], band01[:, lo_:512], ALU.mult, r=[("Ypbb", pi)], w=[("Ypbb", pi)])

                        def b_s2(i):
                            bi, slot, r_, qb = flat[i]
                            has_prev = qb > 0
                            pi = sst[i]
                            pb = PBB[pi]
                            kk_ = KO_B
                            for isden in (False, True):
                                for e_ in range(2):
                                    c0 = e_ * 128
                                    oc = (256 if isden else 0) + slot * 128
                                    o_ap = ps[kk_][e_ * 64:(e_ + 1) * 64, oc:oc + 128]
                                    blkc = r_ * nb + qb
                                    if has_prev:
                                        l1 = ones[:, 0:64] if isden else VGH[:, blkc - 1, e_ * 64:(e_ + 1) * 64]
                                        MM(o_ap, l1, pb[:, c0:c0 + 128], True, False, r=[("Ypbb", pi)] + vg_all, w=[PS(kk_)])
                                    l2 = ones[:, 0:64] if isden else VGH[:, blkc, e_ * 64:(e_ + 1) * 64]
                                    MM(o_ap, l2, pb[:, 256 + c0:256 + c0 + 128], not has_prev, True, r=[("Ypbb", pi)] + vg_all, w=[PS(kk_)])
                            if slot == len(batches[bi]) - 1:
                                blks = batches[bi]
                                if g < 2:
                                    r0_, b0 = blks[0]
                                    t0 = b0 * 128 * dil + r0_
                                    nview = NUM[:, t0:t0 + 255 * dil + 1:dil]
                                    dview = DEN[:, t0:t0 + 255 * dil + 1:dil]
                                    pso, psd = ps[kk_][:, 0:256], ps[kk_][:, 256:512]
                                else:
                                    r0 = blks[0][0]
                                    nview = NUM.rearrange("p (i r) -> p r i", r=16)[:, r0:r0 + 2, :]
                                    dview = DEN.rearrange("p (i r) -> p r i", r=16)[:, r0:r0 + 2, :]
                                    pso = ps[kk_][:, 0:256].rearrange("p (a c) -> p a c", a=2)
                                    psd = ps[kk_][:, 256:512].rearrange("p (a c) -> p a c", a=2)
                                if g == 0:
                                    CP("dve", nview, pso, r=[PS(kk_)], w=["Ynum"])
                                    CP("dve", dview, psd, r=[PS(kk_)], w=["Yden"])
                                else:
                                    TT("dve", nview, nview, pso, ALU.add, r=[PS(kk_), "Ynum"], w=["Ynum"])
                                    TT("dve", dview, dview, psd, ALU.add, r=[PS(kk_), "Yden"], w=["Yden"])

                        for i in range(len(flat) + 1):
                            if i < len(flat):
                                b_s1(i)
                            if i >= 1:
                                b_s2(i - 1)
                            yield
                    for tc in range(4):
                        tsl = slice(tc * 512, (tc + 1) * 512)
                        b = tc % 2
                        RECIP(T1[b], DEN[:, tsl], r=["Yden"], w=[("Yt1", b)])
                        TT("dve", OBT[:, hp, tsl], NUM[:, tsl], T1[b], ALU.mult, r=["Ynum", ("Yt1", b)], w=[("Yobt", hp, tc)])
                    yield

            def thread_X():
                QTAF = A16(20480, 2048)
                KTAF = A16(22528, 2048)
                VAF = A16(24576, 2048).rearrange("p (k t) -> p k t", k=16)
                XW = [A16(26624 + i * 1024, 1024) for i in range(3)]
                E32 = [A32(29696 + i * 1024, 512) for i in range(3)]
                SP = [A16(32768 + i * 512, 512) for i in range(3)]
                WB = [A16(34304 + i * 512, 512) for i in range(3)]
                SACC = [[A16(35840 + (e_ * 2 + pp) * 512, 512) for pp in range(2)] for e_ in range(2)]
                KO = 4
                ui = 0
                for fc in range(4):
                    for which in range(3):
                        src = wbf_d[:, OFF_IN + which * T8: OFF_IN + (which + 1) * T8].rearrange("p (k c) -> p k c", k=8)[:, :, fc * 128:(fc + 1) * 128]
                        DMA(XW[which].rearrange("p (k c) -> p k c", k=8), src, r=wbf_res(OFF_IN + which * T8, T8), w=[("Xw", which)], name=f"w{6 + which}")
                    for which, dst in ((0, QTAF), (1, KTAF)):
                        for tc in range(4):
                            tsl = slice(tc * 512, (tc + 1) * 512)
                            k = XB()
                            for kc in range(8):
                                MM(ps[k][:, :], XW[which][:, kc * 128:(kc + 1) * 128], hT[:, kc, tsl], kc == 0, kc == 7, r=[("Xw", which)], w=[PS(k)])
                            if which == 0:
                                TS("dve", dst[:, tsl], ps[k][:, :], 0.125, ALU.mult, r=[PS(k)], w=[("Xqk", which, tc)])
                            else:
                                CP("dve", dst[:, tsl], ps[k][:, :], r=[PS(k)], w=[("Xqk", which, tc)])
                            yield
                    for bb in range(4):
                        k = XB()
                        for j4 in range(4):
                            tb = bb * 4 + j4
                            for kc in range(8):
                                MM(ps[k][:, j4 * 128:(j4 + 1) * 128], hT[:, kc, tb * 128:(tb + 1) * 128], XW[2][:, kc * 128:(kc + 1) * 128],
                                   kc == 0, kc == 7, r=[("Xw", 2)], w=[PS(k)])
                        CP("dve", VAF[:, 4 * bb:4 * bb + 4, :], ps[k][:, :].rearrange("p (a c) -> p a c", a=4), r=[PS(k)], w=[("Xva", bb)])
                        yield
                    S.fence("X")
                    for c in range(4):
                        ko = KO
                        top = 4 * c + 3
                        for e_ in range(2):
                            MM(ps[ko][e_ * 64:(e_ + 1) * 64, :], zeros[:, 0:64], QTAF[:, 0:512], True, True, r=[], w=[PS(ko)])
                        ulist = [(kb, e_) for kb in range(top, -1, -1) for e_ in range(2)]
                        ust = {}

                        def a_qk(u):
                            nonlocal ui
                            kb, e_ = ulist[u]
                            kx = XB()
                            j = kb - 4 * c
                            lo = 128 * j if j >= 0 else 0
                            ti = ui % 3
                            ui += 1
                            ust[u] = (kx, lo, ti)
                            pt = slice(e_ * 64, e_ * 64 + 64)
                            q0 = c * 512
                            if j >= 0:
                                MM(ps[kx][:, lo:512], KTAF[pt, kb * 128:(kb + 1) * 128], QTAF[pt, q0 + lo:q0 + 512], True, False,
                                   r=[], w=[PS(kx)], skip_group_check=True)
                                MM(ps[kx][:, lo:lo + 128], ident, negmaskA, False, True, r=[], w=[PS(kx)], skip_group_check=True)
                            else:
                                MM(ps[kx][:, :], KTAF[pt, kb * 128:(kb + 1) * 128], QTAF[pt, q0:q0 + 512], True, True, r=[], w=[PS(kx)])

                        def a_exp(u):
                            kx, lo, ti = ust[u]
                            ACT(E32[ti][:, lo:512], ps[kx][:, lo:512], AF.Exp, r=[PS(kx)], w=[("Xe32", ti)])

                        def a_ln(u):
                            kx, lo, ti = ust[u]
                            ACT(SP[ti][:, lo:512], E32[ti][:, lo:512], AF.Ln, r=[("Xe32", ti)], w=[("Xsp", ti)], bias=1.0)

                        def a_tri(u):
                            kb, e_ = ulist[u]
                            kx, lo, ti = ust[u]
                            pp = (top - kb) % 2
                            first = kb == top
                            MM(ps[kx][:, lo:512], negTri, SP[ti][:, lo:512], False, True, r=[("Xsp", ti)], w=[PS(kx)], skip_group_check=True)
                            if not first:
                                MM(ps[kx][:, lo:512], negones, SACC[e_][pp][:, lo:512], False, True, r=[("Xsacc", e_, pp)], w=[PS(kx)],
                                   skip_group_check=True)
                            if kb > 0:
                                nxt = SACC[e_][1 - pp]
                                if lo > 0:
                                    MEMSET("pool", nxt[:, 0:lo], 0.0, w=[("Xsacc", e_, 1 - pp)])
                                if first:
                                    CP("pool", nxt[:, lo:512], SP[ti][:, lo:512], r=[("Xsp", ti)], w=[("Xsacc", e_, 1 - pp)])
                                else:
                                    TT("dve", nxt[:, lo:512], SACC[e_][pp][:, lo:512], SP[ti][:, lo:512], ALU.add,
                                       r=[("Xsp", ti), ("Xsacc", e_, pp)], w=[("Xsacc", e_, 1 - pp)])

                        def a_expw(u):
                            kx, lo, ti = ust[u]
                            ACT(WB[ti][:, lo:512], ps[kx][:, lo:512], AF.Exp, r=[PS(kx)], w=[("Xwb", ti)])

                        def a_pv(u):
                            kb, e_ = ulist[u]
                            kx, lo, ti = ust[u]
                            MM(ps[ko][e_ * 64:(e_ + 1) * 64, lo:512], VAF[:, kb, e_ * 64:(e_ + 1) * 64], WB[ti][:, lo:512], False, kb == 0,
                               r=[("Xwb", ti)], w=[PS(ko)], skip_group_check=True)

                        nu = len(ulist)
                        for t in range(nu + 4):
                            if 0 <= t - 2 < nu:
                                a_tri(t - 2)
                            if 0 <= t - 4 < nu:
                                a_pv(t - 4)
                            if 0 <= t - 1 < nu:
                                a_ln(t - 1)
                            if 0 <= t - 3 < nu:
                                a_expw(t - 3)
                            if t < nu:
                                a_qk(t)
                                a_exp(t)
                            yield
                        CP("dve", OAT[:, fc, c * 512:(c + 1) * 512], ps[ko][:, :], r=[PS(ko)], w=[("Xoat", fc, c)])
                    S.fence("X")

            def run_threads():
                counts = {}
                S.dry = True
                for nm, th in (("X", thread_X), ("Y", thread_Y)):
                    n = 0
                    for _ in th():
                        n += 1
                    counts[nm] = n
                S.dry = False
                xbank.i = 0
                ybank.i = 0
                gx, gy = thread_X(), thread_Y()
                nx, ny = counts["X"], counts["Y"]
                dx = dy = 0
                alive_x = alive_y = True
                while alive_x or alive_y:
                    if alive_x and (SEQ_THREADS or not alive_y or dx * ny <= dy * nx):
                        S.thread = "X"
                        try:
                            next(gx)
                            dx += 1
                        except StopIteration:
                            alive_x = False
                    else:
                        S.thread = "Y"
                        try:
                            next(gy)
                            dy += 1
                        except StopIteration:
                            alive_y = False
                S.thread = None

            run_threads()
            S.barrier()
            if dbg and s == 0:
                DMA(dbg_d["oc"], A16(0, 8192), r=[], w=[], name="dbg")
                DMA(dbg_d["ob"], A16(8192, 4096), r=[], w=[], name="dbg")
                DMA(dbg_d["oa"], A16(12288, 8192), r=[], w=[], name="dbg")
                DMA(dbg_d["xq"], A16(20480, 2048), r=[], w=[], name="dbg")
                DMA(dbg_d["xk"], A16(22528, 2048), r=[], w=[], name="dbg")
                DMA(dbg_d["xv"], A16(24576, 2048), r=[], w=[], name="dbg")

            S.phase = f"s{s}_M1"
            MG = A16(20480, 16384).rearrange("p (k t) -> p k t", k=8)
            ring = WRing([36864, 36864 + MT], MT)
            SG = [A32(45568 + i * 1024, 512) for i in range(2)]
            MA = [A32(47616 + i * 1024, 512) for i in range(2)]
            MU = [A32(49664 + i * 1024, 512) for i in range(2)]
            M2 = [A32(51712 + i * 1024, 512) for i in range(2)]
            srcs = ((OAT, 4, 0), (OBT, 2, 4), (OCT, 4, 6))
            n = 0
            for fo in range(8):
                wt, rw = ring.load(OFF_M + fo * MT, MT)
                for tc in range(4):
                    tsl = slice(tc * 512, (tc + 1) * 512)
                    b = n % 2
                    n += 1
                    for j, (src, nec, eoff) in enumerate(srcs):
                        ky, kg = bank.next(), bank.next()
                        for ec in range(nec):
                            c0 = 3072 + (eoff + ec) * 128
                            MM(ps[ky][:, :], wt[:, c0:c0 + 128], src[:, ec, tsl], ec == 0, ec == nec - 1, r=[rw], w=[PS(ky)])
                        for kc in range(8):
                            c0 = kc * 384 + j * 128
                            MM(ps[kg][:, :], wt[:, c0:c0 + 128], hT[:, kc, tsl], kc == 0, kc == 7, r=[rw], w=[PS(kg)])
                        bcol = cf[:, C_BG + j * 8 + fo: C_BG + j * 8 + fo + 1]
                        sb = (3 * n + j) % 2
                        ACT(SG[sb], ps[kg][:, :], AF.Sigmoid, r=[PS(kg)], w=[("sg", sb)], bias=bcol)
                        if j == 0:
                            TT("dve", MA[b], SG[sb], ps[ky][:, :], ALU.mult, r=[("sg", sb), PS(ky)], w=[("ma", b)])
                        elif j == 1:
                            TT("dve", MU[b], SG[sb], ps[ky][:, :], ALU.mult, r=[("sg", sb), PS(ky)], w=[("mu", b)])
                            TT("pool", M2[b], MA[b], MU[b], ALU.add, r=[("ma", b), ("mu", b)], w=[("m2", b)])
                        else:
                            TT("dve", MU[b], SG[sb], ps[ky][:, :], ALU.mult, r=[("sg", sb), PS(ky)], w=[("mu", b)])
                            TT("pool", MG[:, fo, tsl], M2[b], MU[b], ALU.add, r=[("m2", b), ("mu", b)], w=[("mg", fo, tc)])
            S.barrier()
            if dbg and s == 0:
                DMA(dbg_d["mg"], A16(20480, 16384), r=[], w=[], name="dbg")

            S.phase = f"s{s}_M2"
            WO = A16(0, 8192).rearrange("p (a c) -> p a c", a=2)
            DMA(WO[:, 0, :], wbf_d[:, OFF_WO:OFF_WO + T8], r=wbf_res(OFF_WO, T8), w=["wo0"], name="w0")
            DMA(WO[:, 1, :], wbf_d[:, OFF_WO + T8:OFF_WO + 2 * T8], r=wbf_res(OFF_WO + T8, T8), w=["wo1"], name="w1")
            XT = [A32(8192 + i * 2048, 1024) for i in range(2)]
            TTB = [A32(12288 + i * 2048, 1024) for i in range(2)]
            X1 = [A32(16384 + i * 2048, 1024) for i in range(2)]
            HN = [A16(36864 + i * 1024, 1024) for i in range(2)]
            junk = A16(38912, 1024)
            sm = A32(39936, 128)
            gT = cf[:, C_GTPF:C_GTPF + 8].unsqueeze(2).broadcast_to([128, 8, 128])
            m2st = {}

            def m2_a(tb):
                b = tb % 2
                o = (tb % 4) * 16
                rows = slice(s * SEQ + tb * 128, s * SEQ + (tb + 1) * 128)
                DMA(XT[b], x_d[rows, :], r=[], w=[("xt", b)], name=f"x{b}")
                kh = [bank.next(), bank.next()]
                m2st[tb] = kh
                S.add("act", lambda e, a_=sm[:, o:o + 2]: e.memzero(a_), w=[("ssq", tb % 4, 0), ("ssq", tb % 4, 1)])
                S.add("act", lambda e, a_=sm[:, o + 8:o + 9]: e.memzero(a_), w=[("ssq3", tb % 4)])
                for half in range(2):
                    for fo in range(8):
                        MM(ps[kh[half]][:, :], MG[:, fo, tb * 128:(tb + 1) * 128], WO[:, half, fo * 512:(fo + 1) * 512], fo == 0, fo == 7,
                           r=["wo0", "wo1"], w=[PS(kh[half])])

            def m2_a2(tb):
                o = (tb % 4) * 16
                kh = m2st[tb]
                for half in range(2):
                    ACT(junk[:, 0:512], ps[kh[half]][:, :], AF.Square, r=[PS(kh[half])], w=["junk", ("ssq", tb % 4, half)],
                        accum_out=sm[:, o + half:o + half + 1])

            def m2_b1(tb):
                b = tb % 2
                o = (tb % 4) * 16
                kh = m2st[tb]
                rows = slice(s * SEQ + tb * 128, s * SEQ + (tb + 1) * 128)
                TT("dve", sm[:, o + 2:o + 3], sm[:, o:o + 1], sm[:, o + 1:o + 2], ALU.add, r=[("ssq", tb % 4, 0), ("ssq", tb % 4, 1)], w=[("ssqt", tb % 4)])
                rstd_from_ssq(sm[:, o + 2:o + 3], sm[:, o + 3:o + 4], sm[:, o + 4:o + 5], r=[("ssqt", tb % 4)], w=("rstd", tb % 4))
                for half in range(2):
                    hs = slice(half * 512, (half + 1) * 512)
                    STT("dve", TTB[b][:, hs], ps[kh[half]][:, :], sm[:, o + 4:o + 5], gpm[:, hs], ALU.mult, ALU.mult,
                        r=[PS(kh[half]), ("rstd", tb % 4)], w=[("ttb", b, half)])
                TT("dve", X1[b], TTB[b], XT[b], ALU.add, r=[("ttb", b, 0), ("ttb", b, 1), ("xt", b)], w=[("x1", b)])
                DMA(out_d[rows, :], X1[b], r=[("x1", b)], w=[], name=f"o{b}", q="pool")

            def m2_b2(tb):
                b = tb % 2
                o = (tb % 4) * 16
                ACT(junk, X1[b], AF.Square, r=[("x1", b)], w=["junk", ("ssq3", tb % 4)], accum_out=sm[:, o + 8:o + 9])
                rstd_from_ssq(sm[:, o + 8:o + 9], sm[:, o + 9:o + 10], sm[:, o + 10:o + 11], r=[("ssq3", tb % 4)], w=("rstd3", tb % 4))
                ACT(HN[b], X1[b], AF.Copy, r=[("x1", b), ("rstd3", tb % 4)], w=[("hn", b)], scale=sm[:, o + 10:o + 11])

            def m2_c(tb):
                b = tb % 2
                k = bank.next()
                psT = ps[k][:, :].bitcast(BF16)
                for kc in range(8):
                    TR(psT[:, kc * 128:(kc + 1) * 128], HN[b][:, kc * 128:(kc + 1) * 128], r=[("hn", b)], w=[PS(k)])
                TT("dve", hT[:, :, tb * 128:(tb + 1) * 128], psT.rearrange("p (k t) -> p k t", k=8), gT, ALU.mult, r=[PS(k)], w=[("h2T", tb)])

            for t in range(NTB + 3):
                if t < NTB:
                    m2_a(t)
                if 0 <= t - 3 < NTB:
                    m2_c(t - 3)
                if 0 <= t - 1 < NTB:
                    m2_b1(t - 1)
                if 0 <= t - 2 < NTB:
                    m2_b2(t - 2)
                if t < NTB:
                    m2_a2(t)
            S.barrier()

            S.phase = f"s{s}_F"
            WFO = A16(0, NJ * 1024).rearrange("p (k c) -> p k c", k=NJ)
            half_n = 11 * 1024
            DMA(A16(0, half_n), wbf_d[:, OFF_FO:OFF_FO + half_n], r=wbf_res(OFF_FO, half_n), w=["wfo0"], name="w3")
            DMA(A16(half_n, half_n), wbf_d[:, OFF_FO + half_n:OFF_FO + 2 * half_n], r=wbf_res(OFF_FO + half_n, half_n), w=["wfo1"], name="w4")
            FT = A16(22528, NJ * 512).rearrange("p (k t) -> p k t", k=NJ)
            ring = WRing([33792, 37888, 41984], T8)
            SL = [A32(46080 + i * 1024, 512) for i in range(2)]
            TTB = [A32(48128 + i * 2048, 1024) for i in range(2)]
            X1 = [A32(52224 + i * 2048, 1024) for i in range(2)]
            OT = [A32(56320 + i * 2048, 1024) for i in range(2)]
            junk = A16(60416, 512)
            sm = A32(60928, 128)
            n = 0
            for tc in range(4):
                tsl = slice(tc * 512, (tc + 1) * 512)
                for t in range(11):
                    wt, rw = ring.load(OFF_FI + t * T8, T8)
                    for jj in range(2):
                        j = 2 * t + jj
                        kg, ku = bank.next(), bank.next()
                        for kc in range(8):
                            c0 = kc * 512 + jj * 128
                            MM(ps[kg][:, :], wt[:, c0:c0 + 128], hT[:, kc, tsl], kc == 0, kc == 7, r=[rw], w=[PS(kg)])
                        for kc in range(8):
                            c0 = kc * 512 + 256 + jj * 128
                            MM(ps[ku][:, :], wt[:, c0:c0 + 128], hT[:, kc, tsl], kc == 0, kc == 7, r=[rw], w=[PS(ku)])
                        b = n % 2
                        n += 1
                        ACT(SL[b], ps[kg][:, :], AF.Silu, r=[PS(kg)], w=[("sl", b)])
                        TT("dve", FT[:, j, :], SL[b], ps[ku][:, :], ALU.mult, r=[("sl", b), PS(ku)], w=[("ft", j)])
                ft_all = [("ft", j) for j in range(NJ)]
                for tb4 in range(4):
                    tb = tc * 4 + tb4
                    b = tb % 2
                    rows = slice(s * SEQ + tb * 128, s * SEQ + (tb + 1) * 128)
                    DMA(X1[b], out_d[rows, :], r=[], w=[("x1", b)], name=f"x{b}")
                    kh = [bank.next(), bank.next()]
                    S.add("act", lambda e, a_=sm[:, 0:2]: e.memzero(a_), w=[("ssq", 0), ("ssq", 1)])
                    for half in range(2):
                        for j in range(NJ):
                            MM(ps[kh[half]][:, :], FT[:, j, tb4 * 128:(tb4 + 1) * 128], WFO[:, j, half * 512:(half + 1) * 512], j == 0, j == NJ - 1,
                               r=ft_all + ["wfo0", "wfo1"], w=[PS(kh[half])])
                        ACT(junk, ps[kh[half]][:, :], AF.Square, r=[PS(kh[half])], w=["junk", ("ssq", half)], accum_out=sm[:, half:half + 1])
                    TT("dve", sm[:, 2:3], sm[:, 0:1], sm[:, 1:2], ALU.add, r=[("ssq", 0), ("ssq", 1)], w=["ssqt"])
                    rstd_from_ssq(sm[:, 2:3], sm[:, 3:4], sm[:, 4:5], r=["ssqt"], w=("rstd",))
                    for half in range(2):
                        hs = slice(half * 512, (half + 1) * 512)
                        STT("dve", TTB[b][:, hs], ps[kh[half]][:, :], sm[:, 4:5], gpf[:, hs], ALU.mult, ALU.mult,
                            r=[PS(kh[half]), ("rstd",)], w=[("ttb", b, half)])
                    TT("dve", OT[b], TTB[b], X1[b], ALU.add, r=[("ttb", b, 0), ("ttb", b, 1), ("x1", b)], w=[("ot", b)])
                    DMA(out_d[rows, :], OT[b], r=[("ot", b)], w=[], name=f"o{b}", q="pool")
            S.barrier()

        S.finalize()
        semtab = {}
        for en in S.ENGS:
            for ep in range(S.nepoch[en]):
                semtab[("eng", en, ep)] = es.enter_context(nc.semaphore(f"s_{en}_{ep}"))
        for name in S.dma_cnt:
            semtab[("dma", name)] = es.enter_context(nc.semaphore(f"d_{name}"))
        block = es.enter_context(nc.Block())

        @block.tensor
        def _(e):
            S.emit("pe", e, semtab, nc if scopes else None)

        @block.scalar
        def _(e):
            S.emit("act", e, semtab, nc if scopes else None)

        @block.vector
        def _(e):
            S.emit("dve", e, semtab, nc if scopes else None)

        @block.gpsimd
        def _(e):
            S.emit("pool", e, semtab, nc if scopes else None)

        @block.sync
        def _(e):
            S.emit("sp", e, semtab, nc if scopes else None)
    return nc, len(S.ops)


def _t8(w):
    return w.reshape(8, 128, 512).transpose(1, 0, 2).reshape(128, 4096)


def pack_weights(inp):
    w_in = inp["w_in"][0]
    tiles = []
    tiles += [_t8(w_in[:, 0:512]), _t8(w_in[:, 512:1024]), _t8(w_in[:, 1024:1536])]
    swap = np.concatenate([np.arange(h * 64 + 32, h * 64 + 64).tolist() + np.arange(h * 64, h * 64 + 32).tolist() for h in range(4)]).astype(np.int64)
    vcols = []
    for g in range(3):
        base = 1536 + g * 768
        q = w_in[:, base:base + 256]
        k = w_in[:, base + 256:base + 512]
        vcols.append(w_in[:, base + 512:base + 768])
        tiles.append(_t8(np.concatenate([q, q[:, swap]], axis=1)))
        tiles.append(_t8(np.concatenate([k, k[:, swap]], axis=1)))
    tiles.append(_t8(np.concatenate([vcols[0], vcols[1]], axis=1)))
    tiles.append(_t8(np.concatenate([vcols[2], np.zeros((1024, 256), np.float32)], axis=1)))
    tiles.append(_t8(w_in[:, 3840:4352]))
    wm = inp["w_mem_kv"][0]
    tiles += [_t8(wm[:, 0:512]), _t8(wm[:, 512:1024])]
    wg = inp["w_gate"][0]
    wbr = np.concatenate([inp["w_br_sb"][0], inp["w_br_dil"][0], inp["w_br_mem"][0]], axis=0)
    for fo in range(8):
        gsel = np.concatenate([wg[:, j * 1024 + fo * 128: j * 1024 + fo * 128 + 128] for j in range(3)], axis=1)
        gpart = gsel.reshape(8, 128, 384).transpose(1, 0, 2).reshape(128, 3072)
        bpart = wbr[:, fo * 128:(fo + 1) * 128].reshape(10, 128, 128).transpose(1, 0, 2).reshape(128, 1280)
        tiles.append(np.concatenate([gpart, bpart], axis=1))
    wo = inp["w_o"][0]
    tiles += [_t8(wo[:, 0:512]), _t8(wo[:, 512:1024])]
    wfi = inp["w_ffn_in"][0]
    for t in range(11):
        tiles.append(_t8(np.concatenate([wfi[:, t * 256:(t + 1) * 256], wfi[:, D_FF + t * 256: D_FF + (t + 1) * 256]], axis=1)))
    wfo = inp["w_ffn_out"][0]
    tiles.append(wfo.reshape(NJ, 128, 1024).transpose(1, 0, 2).reshape(128, NJ * 1024))
    wall = np.ascontiguousarray(np.concatenate(tiles, axis=1), dtype=np.float32)
    assert wall.shape == (128, TOT), wall.shape
    return wall


def pack_consts(inp):
    cfa = np.zeros((128, NCF + NCB), np.float32)
    inv_freq = (np.float32(10000.0) ** (-np.arange(32, dtype=np.float32) * np.float32(2.0) / np.float32(64))).astype(np.float32)
    ang = np.arange(SEQ, dtype=np.float32)[None, :] * inv_freq[:, None]
    cos = np.cos(ang).astype(np.float32)
    sin = np.sin(ang).astype(np.float32)
    p = np.arange(128)
    fi = (p % 64) % 32
    sign = np.where((p % 64) < 32, -1.0, 1.0).astype(np.float32)
    cfa[:, C_COS:C_COS + SEQ] = cos[fi]
    cfa[:, C_SIN:C_SIN + SEQ] = sin[fi] * sign[:, None]
    cfa[:, C_GPM:C_GPM + D] = np.broadcast_to(inp["g_post_mix"][0][None, :], (128, D))
    cfa[:, C_GPF:C_GPF + D] = np.broadcast_to(inp["g_post_ffn"][0][None, :], (128, D))
    cfa[:, C_GTPM:C_GTPM + 8] = inp["g_pre_mix"][0].reshape(8, 128).T
    cfa[:, C_GTPF:C_GTPF + 8] = inp["g_pre_ffn"][0].reshape(8, 128).T
    cfa[:, C_GTMEM:C_GTMEM + 8] = inp["g_mem"][0].reshape(8, 128).T
    cfa[:, C_BG:C_BG + 24] = inp["b_gate"][0].reshape(24, 128).T
    o = NCF
    kk = np.arange(128)[:, None]
    qq = np.arange(128)[None, :]
    cfa[:, o:o + 128] = np.eye(128, dtype=np.float32)
    cfa[:, o + 128:o + 256] = np.where(kk >= qq, -30000.0, 0.0)
    cfa[:, o + 256:o + 384] = np.where(kk >= qq, -1.0, 0.0)
    mprev = np.where(kk < qq, -30000.0, 0.0)
    mcur = np.where(kk > qq, -30000.0, 0.0)
    cfa[:, o + 384:o + 896] = np.concatenate([mprev, mprev, mcur, mcur], axis=1)
    sw = np.arange(128)
    sw = (sw // 64) * 64 + ((sw % 64) + 32) % 64
    pm = np.zeros((128, 128), np.float32)
    pm[sw, np.arange(128)] = 1.0
    cfa[:, o + 896:o + 1024] = pm
    cfa[:, o + 1024:o + 1536] = (cfa[:, o + 384:o + 896] == 0.0).astype(np.float32)
    return cfa


_CACHE = {}


def _get_program(nseq, dbg=False):
    key = (nseq, dbg)
    if key not in _CACHE:
        _CACHE[key] = build_program(nseq, dbg)[0]
    return _CACHE[key]


def kernel(**inputs):
    inp = {k: np.asarray(v) for k, v in inputs.items()}
    x = np.ascontiguousarray(inp["x"], dtype=np.float32)
    mem = np.ascontiguousarray(inp["mem"], dtype=np.float32)
    B = x.shape[0]
    nseq = B // NCORES
    wall = pack_weights(inp)
    cfa = pack_consts(inp)
    nc = _get_program(nseq)
    in_maps = []
    for c in range(NCORES):
        in_maps.append({
            "x": x[c * nseq:(c + 1) * nseq].reshape(nseq * SEQ, D),
            "mem": mem[c * nseq:(c + 1) * nseq].reshape(nseq * MEM, D),
            "wall": wall,
            "cf": cfa,
        })
    res = run_bass_kernel_spmd(nc, in_maps, core_ids=list(range(NCORES)))
    outs = [np.asarray(r["out"]).reshape(nseq, SEQ, D) for r in res.results]
    return np.concatenate(outs, axis=0).astype(np.float32)
```

```python
import contextlib
import numpy as np
import concourse.bass as bass
import concourse.mybir as mybir
from concourse.bass_utils import run_bass_kernel_spmd

F32 = mybir.dt.float32
BF16 = mybir.dt.bfloat16
AF = mybir.ActivationFunctionType
ALU = mybir.AluOpType

SEQ = 2048
D = 1024
MEM = 256
NCORES = 8
SEQ_THREADS = False
NTB = SEQ // 128
EPS = 1e-6
D_FF = 2816
NJ = D_FF // 128

T8 = 4096
OFF_IN = 0
OFF_MEM = 12 * T8
OFF_M = 14 * T8
MT = 8 * 384 + 10 * 128
OFF_WO = OFF_M + 8 * MT
OFF_FI = OFF_WO + 2 * T8
OFF_FO = OFF_FI + 11 * T8
TOT = OFF_FO + NJ * 1024
assert TOT == 167936 and TOT % T8 == 0

C_COS = 0
C_SIN = 2048
C_GPM = 4096
C_GPF = 5120
C_GTPM = 6144
C_GTPF = 6152
C_GTMEM = 6160
C_BG = 6168
NCF = 6192
NCB = 1536


class Sched:
    ENGS = ("pe", "act", "dve", "pool", "sp")
    EPOCH = 20000

    def __init__(self):
        self.ops = []
        self.lastw = {}
        self.readers = {}
        self.dma_cnt = {}
        self.phase = "init"
        self.thread = None
        self.fences = {}
        self.tlast = {}
        self.dry = False

    def fence(self, t):
        if self.dry:
            return
        self.fences[t] = set(self.tlast.get(t, {}).values())

    def add(self, eng, fn, r=(), w=(), dma=None):
        if self.dry:
            return None
        i = len(self.ops)
        deps = set()
        if self.thread is not None:
            deps |= self.fences.get(self.thread, set())
            self.tlast.setdefault(self.thread, {})[(eng, dma)] = i
        for x in r:
            lw = self.lastw.get(x)
            if lw is not None:
                deps.add(lw)
        for x in w:
            lw = self.lastw.get(x)
            if lw is not None:
                deps.add(lw)
            for ri in self.readers.get(x, {}).values():
                deps.add(ri)
        op = dict(eng=eng, fn=fn, deps=deps, dma=dma, sig=None, need=False, phase=self.phase)
        if dma is not None:
            c = self.dma_cnt.get(dma, 0) + 1
            self.dma_cnt[dma] = c
            op["sig"] = ("dma", dma, 16 * c)
            op["need"] = True
        self.ops.append(op)
        for x in w:
            self.lastw[x] = i
            self.readers[x] = {}
        for x in r:
            self.readers.setdefault(x, {})[(eng, dma)] = i
        return i

    def barrier(self):
        last = {}
        for i, op in enumerate(self.ops):
            if op["dma"] is not None and op["dma"].startswith("cv"):
                continue
            last[(op["eng"], op["dma"])] = i
        for e in self.ENGS:
            deps = set(v for k, v in last.items() if not (k[0] == e and k[1] is None))
            self.ops.append(dict(eng=e, fn=None, deps=deps, dma=None, sig=None, need=False))
        self.lastw = {k: v for k, v in self.lastw.items() if isinstance(k, tuple) and k and k[0] == "wbf"}
        self.readers = {}
        self.fences = {}
        self.tlast = {}

    def finalize(self):
        ops = self.ops
        for op in ops:
            for d in op["deps"]:
                dop = ops[d]
                if dop["dma"] is not None:
                    continue
                if dop["eng"] != op["eng"] or dop["eng"] != "pe":
                    dop["need"] = True
        cnt = {e: 0 for e in self.ENGS}
        for op in ops:
            if op["dma"] is None and op["need"] and op["fn"] is not None:
                e = op["eng"]
                cnt[e] += 1
                op["sig"] = ("eng", e, (cnt[e] - 1) // self.EPOCH, (cnt[e] - 1) % self.EPOCH + 1)
        self.nepoch = {e: max(1, (cnt[e] + self.EPOCH - 1) // self.EPOCH) for e in self.ENGS}

    def emit(self, eng_name, e, semtab, nc=None):
        ops = self.ops
        waited = {}
        cur_scope = None
        cur_phase = None
        for op in ops:
            if op["eng"] != eng_name:
                continue
            if nc is not None and op["fn"] is not None and op["phase"] != cur_phase:
                if cur_scope is not None:
                    cur_scope.__exit__(None, None, None)
                cur_phase = op["phase"]
                cur_scope = nc.named_scope(cur_phase)
                cur_scope.__enter__()
            for d in sorted(op["deps"]):
                sig = ops[d]["sig"]
                if sig is None:
                    continue
                if sig[0] == "eng":
                    if sig[1] == eng_name and eng_name == "pe":
                        continue
                    key = ("eng", sig[1], sig[2])
                    val = sig[3]
                else:
                    key = ("dma", sig[1])
                    val = sig[2]
                if waited.get(key, 0) >= val:
                    continue
                waited[key] = val
                e.wait_ge(semtab[key], val)
            if op["fn"] is None:
                continue
            ins = op["fn"](e)
            sig = op["sig"]
            if sig is not None:
                if sig[0] == "eng":
                    ins.then_inc(semtab[("eng", sig[1], sig[2])], 1)
                else:
                    ins.then_inc(semtab[("dma", sig[1])], 16)
        if cur_scope is not None:
            cur_scope.__exit__(None, None, None)
        if eng_name == "sp":
            for name, c in self.dma_cnt.items():
                key = ("dma", name)
                if waited.get(key, 0) < 16 * c:
                    e.wait_ge(semtab[key], 16 * c)


def build_program(nseq, dbg=False, scopes=False):
    nc = bass.Bass("TRN2", target_bir_lowering=False)
    x_d = nc.dram_tensor("x", [nseq * SEQ, D], F32, kind="ExternalInput").ap()
    mem_d = nc.dram_tensor("mem", [nseq * MEM, D], F32, kind="ExternalInput").ap()
    wall_d = nc.dram_tensor("wall", [128, TOT], F32, kind="ExternalInput").ap()
    cf_d = nc.dram_tensor("cf", [128, NCF + NCB], F32, kind="ExternalInput").ap()
    out_d = nc.dram_tensor("out", [nseq * SEQ, D], F32, kind="ExternalOutput").ap()
    wbf_d = nc.dram_tensor("wallbf", [128, TOT], BF16).ap()
    dbg_d = {}
    if dbg:
        for nm, n in (("oa", 8192), ("ob", 4096), ("oc", 8192), ("mg", 16384), ("ht", 16384), ("xq", 2048), ("xk", 2048), ("xv", 2048)):
            dbg_d[nm] = nc.dram_tensor("dbg_" + nm, [128, n], BF16, kind="ExternalOutput").ap()

    S = Sched()
    es = contextlib.ExitStack()
    with es:
        cf = es.enter_context(nc.sbuf_tensor("cf32", [128, NCF], F32))
        cb = es.enter_context(nc.sbuf_tensor("cb16", [128, 1920], BF16))
        hT_t = es.enter_context(nc.sbuf_tensor("hT", [128, 8 * SEQ], BF16))
        AR = 65536
        arena = es.enter_context(nc.sbuf_tensor("arena", [128, AR], BF16))
        ps = [es.enter_context(nc.psum_tensor(f"ps{i}", [128, 512], F32)) for i in range(8)]

        def A16(off, n):
            return arena[:, off:off + n]

        def A32(off, n):
            return arena[:, off:off + 2 * n].bitcast(F32)

        hT = hT_t[:, :].rearrange("p (k t) -> p k t", k=8)
        ident = cb[:, 0:128]
        negmaskA = cb[:, 128:256]
        negTri = cb[:, 256:384]
        negmaskB = cb[:, 384:640]
        negmask2 = cb[:, 384:896]
        perm = cb[:, 896:1024]
        band01 = cb[:, 1024:1536]
        ones = cb[:, 1536:1664]
        negones = cb[:, 1664:1792]
        zeros = cb[:, 1792:1920]
        cosT = cf[:, C_COS:C_COS + 2048]
        sinT = cf[:, C_SIN:C_SIN + 2048]
        gpm = cf[:, C_GPM:C_GPM + 1024]
        gpf = cf[:, C_GPF:C_GPF + 1024]

        def MM(out, lhsT, rhs, start, stop, r, w, **kw):
            S.add("pe", lambda e: e.matmul(out, lhsT=lhsT, rhs=rhs, start=start, stop=stop, **kw), r=r, w=w)

        def TR(out, in_, r, w):
            S.add("pe", lambda e: e.transpose(out, in_, ident), r=r, w=w)

        def ACT(out, in_, func, r, w, **kw):
            S.add("act", lambda e: e.activation(out=out, in_=in_, func=func, **kw), r=r, w=w)

        def DMA(out, in_, r, w, name, q="sp"):
            S.add(q, lambda e: e.dma_start(out=out, in_=in_), r=r, w=w, dma=name)

        def TT(eng, out, in0, in1, op, r, w):
            S.add(eng, lambda e: e.tensor_tensor(out=out, in0=in0, in1=in1, op=op), r=r, w=w)

        def TS(eng, out, in0, s1, op0, r, w, s2=None, op1=None):
            if op1 is None:
                S.add(eng, lambda e: e.tensor_scalar(out=out, in0=in0, scalar1=s1, scalar2=None, op0=op0), r=r, w=w)
            else:
                S.add(eng, lambda e: e.tensor_scalar(out=out, in0=in0, scalar1=s1, scalar2=s2, op0=op0, op1=op1), r=r, w=w)

        def STT(eng, out, in0, scalar, in1, op0, op1, r, w):
            S.add(eng, lambda e: e.scalar_tensor_tensor(out=out, in0=in0, scalar=scalar, in1=in1, op0=op0, op1=op1), r=r, w=w)

        def CP(eng, out, in_, r, w):
            if eng == "act":
                ACT(out, in_, AF.Copy, r, w)
            else:
                S.add(eng, lambda e: e.tensor_copy(out=out, in_=in_), r=r, w=w)

        def MEMSET(eng, ap, val, w):
            S.add(eng, lambda e: e.memset(ap, val), w=w)

        def RECIP(out, in_, r, w):
            S.add("dve", lambda e: e.reciprocal(out=out, in_=in_), r=r, w=w)

        class Bank:
            def __init__(self):
                self.i = 0

            def next(self):
                k = self.i % 8
                self.i += 1
                return k

        bank = Bank()

        def PS(k):
            return ("ps", k)

        class WRing:
            def __init__(self, offs, size):
                self.offs = offs
                self.size = size
                self.i = 0

            def load(self, off, n):
                k = self.i % len(self.offs)
                self.i += 1
                v = A16(self.offs[k], n)
                DMA(v, wbf_d[:, off:off + n], r=wbf_res(off, n), w=[("ws", k)], name=f"w{k}")
                return v, ("ws", k)

        def rstd_from_ssq(ssq, tmp, rstd, r, w):
            ACT(tmp, ssq, AF.Ln, r=r, w=[w + ("ln",)], scale=1.0 / D, bias=EPS)
            ACT(rstd, tmp, AF.Exp, r=[w + ("ln",)], w=[w], scale=-0.5)

        S.phase = "P0"
        DMA(cf[:, :], cf_d[:, 0:NCF], r=[], w=["cf"], name="c0")
        cst32 = A32(0, NCB)
        DMA(cst32, cf_d[:, NCF:NCF + NCB], r=[], w=["cst32"], name="c0")
        CP("dve", cb[:, 0:NCB], cst32, r=["cst32"], w=["cb"])
        MEMSET("pool", ones, 1.0, w=["cb1"])
        MEMSET("pool", negones, -1.0, w=["cb2"])
        MEMSET("pool", zeros, 0.0, w=["cb3"])
        S.barrier()
        def conv(chunks, shared=False):
            idxs = []
            for n_, i in enumerate(chunks):
                nm = f"cvs{n_ % 3}" if shared else f"cv{i}"
                idxs.append(S.add("pool", lambda e, i=i: e.dma_start(out=wbf_d[:, i * T8:(i + 1) * T8], in_=wall_d[:, i * T8:(i + 1) * T8]),
                                  r=[], w=[("wbf", i)], dma=nm))
            if shared:
                for k in idxs:
                    op = S.ops[k]
                    op["sig"] = ("dma", op["dma"], 16 * S.dma_cnt[op["dma"]])

        def wbf_res(off, n):
            return [("wbf", c_) for c_ in range(off // T8, (off + n - 1) // T8 + 1)]

        conv_order = [12, 13, 11, 0, 1, 2, 3, 4, 9, 5, 6, 7, 8, 10] + list(range(14, TOT // T8))
        conv(conv_order[:14])

        OCT = A16(0, 8192).rearrange("p (k t) -> p k t", k=4)
        OBT = A16(8192, 4096).rearrange("p (k t) -> p k t", k=2)
        OAT = A16(12288, 8192).rearrange("p (k t) -> p k t", k=4)

        def norm_transpose(src_rows, nblk, dstT, gT_off, base, small, dname):
            xt = [A32(base + i * 2048, 1024) for i in range(4)]
            hn = [A16(base + 8192, 1024), A16(base + 9216, 1024)]
            junk = A16(base + 10240, 1024)
            gT = cf[:, gT_off:gT_off + 8].unsqueeze(2).broadcast_to([128, 8, 128])
            MEMSET("pool", small[:, 0:16], 0.0, w=[("ssq", t_) for t_ in range(nblk)])
            pst = {}

            def nt_a(tb):
                b = tb % 4
                DMA(xt[b], src_rows(tb), r=[], w=[("xt", b)], name=f"{dname}{b}")
                ACT(junk, xt[b], AF.Square, r=[("xt", b)], w=["junk", ("ssq", tb)], accum_out=small[:, tb:tb + 1])
                rstd_from_ssq(small[:, tb:tb + 1], small[:, 16 + tb:17 + tb], small[:, 32 + tb:33 + tb],
                              r=[("ssq", tb)], w=("rstd", tb))

            def nt_b(tb):
                b = tb % 2
                TS("dve", hn[b], xt[tb % 4], small[:, 32 + tb:33 + tb], ALU.mult, r=[("xt", tb % 4), ("rstd", tb)], w=[("hn", b)])
                k = bank.next()
                pst[tb] = k
                psT = ps[k][:, :].bitcast(BF16)
                for kc in range(8):
                    TR(psT[:, kc * 128:(kc + 1) * 128], hn[b][:, kc * 128:(kc + 1) * 128], r=[("hn", b)], w=[PS(k)])

            def nt_c(tb):
                k = pst[tb]
                psT = ps[k][:, :].bitcast(BF16)
                TT("dve", dstT[:, :, tb * 128:(tb + 1) * 128], psT.rearrange("p (k t) -> p k t", k=8), gT, ALU.mult,
                   r=[PS(k)], w=[("dstT", tb)])

            for t in range(nblk + 2):
                if 0 <= t - 2 < nblk:
                    nt_c(t - 2)
                if 0 <= t - 1 < nblk:
                    nt_b(t - 1)
                if t < nblk:
                    nt_a(t)

        def evac(i, out, in_, r, w):
            CP("act" if i % 2 == 0 else "dve", out, in_, r, w)

        for s in range(nseq):
            S.phase = f"s{s}_P1"
            L0 = 20480
            KMT = A16(37888, 1024).rearrange("p (k t) -> p k t", k=4)
            VM = A16(38912, 1024).rearrange("p (k t) -> p k t", k=2)
            MEMT = A16(56320, 2048).rearrange("p (k t) -> p k t", k=8)
            small = A32(L0 + 11264, 64)
            ring = WRing([41984, 41984 + MT], MT)
            wk, rk = ring.load(OFF_MEM, T8)
            wv, rv = ring.load(OFF_MEM + T8, T8)
            norm_transpose(lambda tb: mem_d[s * MEM + tb * 128: s * MEM + (tb + 1) * 128, :], 2, MEMT, C_GTMEM, L0, small, "x")
            norm_transpose(lambda tb: x_d[s * SEQ + tb * 128: s * SEQ + (tb + 1) * 128, :], NTB, hT, C_GTPM, L0, small, "x")
            for hc in range(4):
                k = bank.next()
                for kc in range(8):
                    MM(ps[k][:, 0:256], wk[:, kc * 512 + hc * 128: kc * 512 + hc * 128 + 128], MEMT[:, kc, :], kc == 0, kc == 7,
                       r=[rk, ("dstT", 0), ("dstT", 1)], w=[PS(k)])
                evac(hc, KMT[:, hc, :], ps[k][:, 0:256], r=[PS(k)], w=[("kmt", hc)])
            for mb in range(2):
                k = bank.next()
                for kc in range(8):
                    MM(ps[k][:, :], MEMT[:, kc, mb * 128:(mb + 1) * 128], wv[:, kc * 512:(kc + 1) * 512], kc == 0, kc == 7,
                       r=[rv, ("dstT", 0), ("dstT", 1)], w=[PS(k)])
                evac(mb, VM[:, mb, :], ps[k][:, :], r=[PS(k)], w=[("vm", mb)])
            S.barrier()
            if dbg and s == 0:
                DMA(dbg_d["ht"], hT_t[:, :], r=[], w=[], name="dbg")

            if s == 0:
                conv(conv_order[14:], shared=True)
            S.phase = f"s{s}_XY"
            xbank = Bank()
            ybank = Bank()

            def XB():
                k = xbank.i % 4
                xbank.i += 1
                return k

            def YB():
                k = 5 + ybank.i % 3
                ybank.i += 1
                return k

            def thread_Y():
                QCT = A16(39936, 8192).rearrange("p (k t) -> p k t", k=4)
                wq = A16(48128, T8)
                PB = [A16(52224 + i * 512, 512) for i in range(4)]
                RD = [A32(54272 + i * 1024, 512) for i in range(2)]
                DMA(wq, wbf_d[:, OFF_IN + 11 * T8: OFF_IN + 12 * T8], r=wbf_res(OFF_IN + 11 * T8, T8), w=["Ywq"], name="w2")
                for fc in range(4):
                    for tc in range(4):
                        k = YB()
                        for kc in range(8):
                            MM(ps[k][:, :], wq[:, kc * 512 + fc * 128: kc * 512 + fc * 128 + 128], hT[:, kc, tc * 512:(tc + 1) * 512],
                               kc == 0, kc == 7, r=["Ywq"], w=[PS(k)])
                        CP("dve", QCT[:, fc, tc * 512:(tc + 1) * 512], ps[k][:, :], r=[PS(k)], w=[("Yqct", fc, tc)])
                        yield
                cscale = float(128 ** -0.5)
                u = 0
                for h in range(4):
                    for tc in range(4):
                        k0, k1 = YB(), YB()
                        for mb, k in ((0, k0), (1, k1)):
                            MM(ps[k][:, :], KMT[:, h, mb * 128:(mb + 1) * 128], QCT[:, h, tc * 512:(tc + 1) * 512], True, True,
                               r=[("Yqct", h, tc)], w=[PS(k)])
                        pi0, pi1 = (2 * u) % 4, (2 * u + 1) % 4
                        ACT(PB[pi0], ps[k0][:, :], AF.Exp, r=[PS(k0)], w=[("Ypb", pi0)], scale=cscale)
                        ACT(PB[pi1], ps[k1][:, :], AF.Exp, r=[PS(k1)], w=[("Ypb", pi1)], scale=cscale)
                        yield
                        ko, kd = YB(), YB()
                        for mb, pi in ((0, pi0), (1, pi1)):
                            MM(ps[ko][:, :], VM[:, mb, h * 128:(h + 1) * 128], PB[pi], mb == 0, mb == 1, r=[("Ypb", pi)], w=[PS(ko)])
                        for mb, pi in ((0, pi0), (1, pi1)):
                            MM(ps[kd][:, :], ones, PB[pi], mb == 0, mb == 1, r=[("Ypb", pi)], w=[PS(kd)])
                        rd = RD[u % 2]
                        RECIP(rd, ps[kd][:, :], r=[PS(kd)], w=[("Yrd", u % 2)])
                        TT("dve", OCT[:, h, tc * 512:(tc + 1) * 512], ps[ko][:, :], rd, ALU.mult, r=[PS(ko), ("Yrd", u % 2)], w=[("Yoct", h, tc)])
                        u += 1
                        yield
                S.fence("Y")
                QZH = A16(37888, 4096).rearrange("p (e t) -> p e t", e=2)
                MEMSET("pool", QZH[64:128, 0, :], 0.0, w=["Yqz0"])
                MEMSET("pool", QZH[0:64, 1, :], 0.0, w=["Yqz1"])
                KTH = A16(41984, 2048)
                VGH = A16(44032, 2048).rearrange("p (k t) -> p k t", k=16)
                NUM = A32(46080, 2048)
                DEN = A32(50176, 2048)
                WQ = [A16(54272, 1024), A16(55296, 1024)]
                XB16 = [A16(56320, 512), A16(56832, 512)]
                WV = A16(58368, 1024)
                T1 = [A32(59392 + i * 1024, 512) for i in range(2)]
                T2 = [A32(61440 + i * 1024, 512) for i in range(2)]
                PBB = [A16(63488 + i * 512, 512) for i in range(3)]
                nrope = 0
                npb = 0
                for hp in range(2):
                    for g, dil in enumerate((1, 4, 16)):
                        nb = NTB // dil
                        for which in range(2):
                            src = wbf_d[:, OFF_IN + (3 + 2 * g + which) * T8: OFF_IN + (4 + 2 * g + which) * T8]
                            src = src.rearrange("p (k c) -> p k c", k=8)[:, :, hp * 128:(hp + 1) * 128]
                            DMA(WQ[which].rearrange("p (k c) -> p k c", k=8), src, r=wbf_res(OFF_IN + (3 + 2 * g + which) * T8, T8), w=[("Ywq", which)], name=f"w{3 + which}")
                        vt_off = OFF_IN + (9 if g < 2 else 10) * T8
                        voff = (256 if g == 1 else 0) + hp * 128
                        srcv = wbf_d[:, vt_off:vt_off + T8].rearrange("p (k c) -> p k c", k=8)[:, :, voff:voff + 128]
                        DMA(WV.rearrange("p (k c) -> p k c", k=8), srcv, r=wbf_res(vt_off, T8), w=["Ywv"], name="w5")
                        rtiles = [(which, tc) for which in range(2) for tc in range(4)]
                        rst = {}

                        def r_s1(i):
                            nonlocal nrope
                            which, tc = rtiles[i]
                            wt = WQ[which]
                            tsl = slice(tc * 512, (tc + 1) * 512)
                            k1 = YB()
                            for kc in range(8):
                                MM(ps[k1][:, :], wt[:, kc * 128:(kc + 1) * 128], hT[:, kc, tsl], kc == 0, kc == 7, r=[("Ywq", which)], w=[PS(k1)])
                            b = nrope % 2
                            nrope += 1
                            rst[i] = b
                            CP("dve", XB16[b], ps[k1][:, :], r=[PS(k1)], w=[("Yxb", b)])
                            TT("dve", T1[b], ps[k1][:, :], cosT[:, tsl], ALU.mult, r=[PS(k1)], w=[("Yt1", b)])

                        def r_s2(i):
                            which, tc = rtiles[i]
                            tsl = slice(tc * 512, (tc + 1) * 512)
                            b = rst[i]
                            k2 = YB()
                            MM(ps[k2][:, :], perm, XB16[b], True, True, r=[("Yxb", b)], w=[PS(k2)])
                            TT("dve", T2[b], ps[k2][:, :], sinT[:, tsl], ALU.mult, r=[PS(k2)], w=[("Yt2", b)])
                            if which == 0:
                                TT("dve", QZH[0:64, 0, tsl], T1[b][0:64, :], T2[b][0:64, :], ALU.add, r=[("Yt1", b), ("Yt2", b), "Yqz0"],
                                   w=[("Yqk", which, tc)])
                                TT("dve", QZH[64:128, 1, tsl], T1[b][64:128, :], T2[b][64:128, :], ALU.add, r=[("Yt1", b), ("Yt2", b), "Yqz1"],
                                   w=[("Yqk2", tc)])
                            else:
                                TT("dve", KTH[:, tsl], T1[b], T2[b], ALU.add, r=[("Yt1", b), ("Yt2", b)], w=[("Yqk", which, tc)])

                        for i in range(len(rtiles) + 1):
                            if i < len(rtiles):
                                r_s1(i)
                            if i >= 1:
                                r_s2(i - 1)
                            yield
                        for bb in range(4):
                            k = YB()
                            for j4 in range(4):
                                blk = bb * 4 + j4
                                r_, b_ = blk // nb, blk % nb
                                t0 = b_ * 128 * dil + r_
                                for kc in range(8):
                                    MM(ps[k][:, j4 * 128:(j4 + 1) * 128], hT[:, kc, t0:t0 + 127 * dil + 1:dil], WV[:, kc * 128:(kc + 1) * 128],
                                       kc == 0, kc == 7, r=["Ywv"], w=[PS(k)])
                            CP("dve", VGH[:, 4 * bb:4 * bb + 4, :], ps[k][:, :].rearrange("p (a c) -> p a c", a=4), r=[PS(k)], w=[("Yvg", bb)])
                            yield
                        qk_all = [("Yqk", wh, tc) for wh in range(2) for tc in range(4)] + [("Yqk2", tc) for tc in range(4)]
                        vg_all = [("Yvg", bb) for bb in range(4)]

                        def tok(r_, b_):
                            t0 = b_ * 128 * dil + r_
                            return slice(t0, t0 + 127 * dil + 1, dil)

                        batches = []
                        if g < 2:
                            for r_ in range(dil):
                                for b0 in range(0, nb, 2):
                                    batches.append([(r_, b0 + i) for i in range(2)])
                        else:
                            for r0 in range(0, 16, 2):
                                batches.append([(r0 + i, 0) for i in range(2)])
                        KO_B = 7
                        flat = [(bi, slot, r_, qb) for bi, blks in enumerate(batches) for slot, (r_, qb) in enumerate(blks)]
                        sst = {}

                        def b_s1(i):
                            nonlocal npb
                            bi, slot, r_, qb = flat[i]
                            ks = 5 + i % 2
                            has_prev = qb > 0
                            qz = QZH[:, :, tok(r_, qb)]
                            if has_prev:
                                MM(ps[ks][:, 0:256], KTH[:, tok(r_, qb - 1)], qz, True, True, r=qk_all, w=[PS(ks)])
                            MM(ps[ks][:, 256:512], KTH[:, tok(r_, qb)], qz, True, True, r=qk_all, w=[PS(ks)])
                            pi = npb % 3
                            npb += 1
                            sst[i] = pi
                            pb = PBB[pi]
                            lo_ = 0 if has_prev else 256
                            ACT(pb[:, lo_:512], ps[ks][:, lo_:512], AF.Exp, r=[PS(ks)], w=[("Ypbb", pi)], scale=0.125)
                            TT("dve", pb[:, lo_:512], pb[:, lo_:512], band01[:, lo_:512], ALU.mult, r=[("Ypbb", pi)], w=[("Ypbb", pi)])

                        def b_s2(i):
                            bi, slot, r_, qb = flat[i]
                            has_prev = qb > 0
                            pi = sst[i]
                            pb = PBB[pi]
                            kk_ = KO_B
                            for isden in (False, True):
                                for e_ in range(2):
                                    c0 = e_ * 128
                                    oc = (256 if isden else 0) + slot * 128
                                    o_ap = ps[kk_][e_ * 64:(e_ + 1) * 64, oc:oc + 128]
                                    blkc = r_ * nb + qb
                                    if has_prev:
                                        l1 = ones[:, 0:64] if isden else VGH[:, blkc - 1, e_ * 64:(e_ + 1) * 64]
                                        MM(o_ap, l1, pb[:, c0:c0 + 128], True, False, r=[("Ypbb", pi)] + vg_all, w=[PS(kk_)])
                                    l2 = ones[:, 0:64] if isden else VGH[:, blkc, e_ * 64:(e_ + 1) * 64]
                                    MM(o_ap, l2, pb[:, 256 + c0:256 + c0 + 128], not has_prev, True, r=[("Ypbb", pi)] + vg_all, w=[PS(kk_)])
                            if slot == len(batches[bi]) - 1:
                                blks = batches[bi]
                                if g < 2:
                                    r0_, b0 = blks[0]
                                    t0 = b0 * 128 * dil + r0_
                                    nview = NUM[:, t0:t0 + 255 * dil + 1:dil]
                                    dview = DEN[:, t0:t0 + 255 * dil + 1:dil]
                                    pso, psd = ps[kk_][:, 0:256], ps[kk_][:, 256:512]
                                else:
                                    r0 = blks[0][0]
                                    nview = NUM.rearrange("p (i r) -> p r i", r=16)[:, r0:r0 + 2, :]
                                    dview = DEN.rearrange("p (i r) -> p r i", r=16)[:, r0:r0 + 2, :]
                                    pso = ps[kk_][:, 0:256].rearrange("p (a c) -> p a c", a=2)
                                    psd = ps[kk_][:, 256:512].rearrange("p (a c) -> p a c", a=2)
                                if g == 0:
                                    CP("dve", nview, pso, r=[PS(kk_)], w=["Ynum"])
                                    CP("dve", dview, psd, r=[PS(kk_)], w=["Yden"])
                                else:
                                    TT("dve", nview, nview, pso, ALU.add, r=[PS(kk_), "Ynum"], w=["Ynum"])
                                    TT("dve", dview, dview, psd, ALU.add, r=[PS(kk_), "Yden"], w=["Yden"])

                        for i in range(len(flat) + 1):
                            if i < len(flat):
                                b_s1(i)
                            if i >= 1:
                                b_s2(i - 1)
                            yield
                    for tc in range(4):
                        tsl = slice(tc * 512, (tc + 1) * 512)
                        b = tc % 2
                        RECIP(T1[b], DEN[:, tsl], r=["Yden"], w=[("Yt1", b)])
                        TT("dve", OBT[:, hp, tsl], NUM[:, tsl], T1[b], ALU.mult, r=["Ynum", ("Yt1", b)], w=[("Yobt", hp, tc)])
                    yield

            def thread_X():
                QTAF = A16(20480, 2048)
                KTAF = A16(22528, 2048)
                VAF = A16(24576, 2048).rearrange("p (k t) -> p k t", k=16)
                XW = [A16(26624 + i * 1024, 1024) for i in range(3)]
                E32 = [A32(29696 + i * 1024, 512) for i in range(3)]
                SP = [A16(32768 + i * 512, 512) for i in range(3)]
                WB = [A16(34304 + i * 512, 512) for i in range(3)]
                SACC = [[A16(35840 + (e_ * 2 + pp) * 512, 512) for pp in range(2)] for e_ in range(2)]
                KO = 4
                ui = 0
                for fc in range(4):
                    for which in range(3):
                        src = wbf_d[:, OFF_IN + which * T8: OFF_IN + (which + 1) * T8].rearrange("p (k c) -> p k c", k=8)[:, :, fc * 128:(fc + 1) * 128]
                        DMA(XW[which].rearrange("p (k c) -> p k c", k=8), src, r=wbf_res(OFF_IN + which * T8, T8), w=[("Xw", which)], name=f"w{6 + which}")
                    for which, dst in ((0, QTAF), (1, KTAF)):
                        for tc in range(4):
                            tsl = slice(tc * 512, (tc + 1) * 512)
                            k = XB()
                            for kc in range(8):
                                MM(ps[k][:, :], XW[which][:, kc * 128:(kc + 1) * 128], hT[:, kc, tsl], kc == 0, kc == 7, r=[("Xw", which)], w=[PS(k)])
                            if which == 0:
                                TS("dve", dst[:, tsl], ps[k][:, :], 0.125, ALU.mult, r=[PS(k)], w=[("Xqk", which, tc)])
                            else:
                                CP("dve", dst[:, tsl], ps[k][:, :], r=[PS(k)], w=[("Xqk", which, tc)])
                            yield
                    for bb in range(4):
                        k = XB()
                        for j4 in range(4):
                            tb = bb * 4 + j4
                            for kc in range(8):
                                MM(ps[k][:, j4 * 128:(j4 + 1) * 128], hT[:, kc, tb * 128:(tb + 1) * 128], XW[2][:, kc * 128:(kc + 1) * 128],
                                   kc == 0, kc == 7, r=[("Xw", 2)], w=[PS(k)])
                        CP("dve", VAF[:, 4 * bb:4 * bb + 4, :], ps[k][:, :].rearrange("p (a c) -> p a c", a=4), r=[PS(k)], w=[("Xva", bb)])
                        yield
                    S.fence("X")
                    for c in range(4):
                        ko = KO
                        top = 4 * c + 3
                        for e_ in range(2):
                            MM(ps[ko][e_ * 64:(e_ + 1) * 64, :], zeros[:, 0:64], QTAF[:, 0:512], True, True, r=[], w=[PS(ko)])
                        ulist = [(kb, e_) for kb in range(top, -1, -1) for e_ in range(2)]
                        ust = {}

                        def a_qk(u):
                            nonlocal ui
                            kb, e_ = ulist[u]
                            kx = XB()
                            j = kb - 4 * c
                            lo = 128 * j if j >= 0 else 0
                            ti = ui % 3
                            ui += 1
                            ust[u] = (kx, lo, ti)
                            pt = slice(e_ * 64, e_ * 64 + 64)
                            q0 = c * 512
                            if j >= 0:
                                MM(ps[kx][:, lo:512], KTAF[pt, kb * 128:(kb + 1) * 128], QTAF[pt, q0 + lo:q0 + 512], True, False,
                                   r=[], w=[PS(kx)], skip_group_check=True)
                                MM(ps[kx][:, lo:lo + 128], ident, negmaskA, False, True, r=[], w=[PS(kx)], skip_group_check=True)
                            else:
                                MM(ps[kx][:, :], KTAF[pt, kb * 128:(kb + 1) * 128], QTAF[pt, q0:q0 + 512], True, True, r=[], w=[PS(kx)])

                        def a_exp(u):
                            kx, lo, ti = ust[u]
                            ACT(E32[ti][:, lo:512], ps[kx][:, lo:512], AF.Exp, r=[PS(kx)], w=[("Xe32", ti)])

                        def a_ln(u):
                            kx, lo, ti = ust[u]
                            ACT(SP[ti][:, lo:512], E32[ti][:, lo:512], AF.Ln, r=[("Xe32", ti)], w=[("Xsp", ti)], bias=1.0)

                        def a_tri(u):
                            kb, e_ = ulist[u]
                            kx, lo, ti = ust[u]
                            pp = (top - kb) % 2
                            first = kb == top
                            MM(ps[kx][:, lo:512], negTri, SP[ti][:, lo:512], False, True, r=[("Xsp", ti)], w=[PS(kx)], skip_group_check=True)
                            if not first:
                                MM(ps[kx][:, lo:512], negones, SACC[e_][pp][:, lo:512], False, True, r=[("Xsacc", e_, pp)], w=[PS(kx)],
                                   skip_group_check=True)
                            if kb > 0:
                                nxt = SACC[e_][1 - pp]
                                if lo > 0:
                                    MEMSET("pool", nxt[:, 0:lo], 0.0, w=[("Xsacc", e_, 1 - pp)])
                                if first:
                                    CP("pool", nxt[:, lo:512], SP[ti][:, lo:512], r=[("Xsp", ti)], w=[("Xsacc", e_, 1 - pp)])
                                else:
                                    TT("dve", nxt[:, lo:512], SACC[e_][pp][:, lo:512], SP[ti][:, lo:512], ALU.add,
                                       r=[("Xsp", ti), ("Xsacc", e_, pp)], w=[("Xsacc", e_, 1 - pp)])

                        def a_expw(u):
                            kx, lo, ti = ust[u]
                            ACT(WB[ti][:, lo:512], ps[kx][:, lo:512], AF.Exp, r=[PS(kx)], w=[("Xwb", ti)])

                        def a_pv(u):
                            kb, e_ = ulist[u]
                            kx, lo, ti = ust[u]
                            MM(ps[ko][e_ * 64:(e_ + 1) * 64, lo:512], VAF[:, kb, e_ * 64:(e_ + 1) * 64], WB[ti][:, lo:512], False, kb == 0,
                               r=[("Xwb", ti)], w=[PS(ko)], skip_group_check=True)

                        nu = len(ulist)
                        for t in range(nu + 4):
                            if 0 <= t - 2 < nu:
                                a_tri(t - 2)
                            if 0 <= t - 4 < nu:
                                a_pv(t - 4)
                            if 0 <= t - 1 < nu:
                                a_ln(t - 1)
                            if 0 <= t - 3 < nu:
                                a_expw(t - 3)
                            if t < nu:
                                a_qk(t)
                                a_exp(t)
                            yield
                        CP("dve", OAT[:, fc, c * 512:(c + 1) * 512], ps[ko][:, :], r=[PS(ko)], w=[("Xoat", fc, c)])
                    S.fence("X")

            def run_threads():
                counts = {}
                S.dry = True
                for nm, th in (("X", thread_X), ("Y", thread_Y)):
                    n = 0
                    for _ in th():
                        n += 1
                    counts[nm] = n
                S.dry = False
                xbank.i = 0
                ybank.i = 0
                gx, gy = thread_X(), thread_Y()
                nx, ny = counts["X"], counts["Y"]
                dx = dy = 0
                alive_x = alive_y = True
                while alive_x or alive_y:
                    if alive_x and (SEQ_THREADS or not alive_y or dx * ny <= dy * nx + 0.06 * nx * ny):
                        S.thread = "X"
                        try:
                            next(gx)
                            dx += 1
                        except StopIteration:
                            alive_x = False
                    else:
                        S.thread = "Y"
                        try:
                            next(gy)
                            dy += 1
                        except StopIteration:
                            alive_y = False
                S.thread = None

            run_threads()
            S.barrier()
            if dbg and s == 0:
                DMA(dbg_d["oc"], A16(0, 8192), r=[], w=[], name="dbg")
                DMA(dbg_d["ob"], A16(8192, 4096), r=[], w=[], name="dbg")
                DMA(dbg_d["oa"], A16(12288, 8192), r=[], w=[], name="dbg")
                DMA(dbg_d["xq"], A16(20480, 2048), r=[], w=[], name="dbg")
                DMA(dbg_d["xk"], A16(22528, 2048), r=[], w=[], name="dbg")
                DMA(dbg_d["xv"], A16(24576, 2048), r=[], w=[], name="dbg")

            S.phase = f"s{s}_M1"
            MG = A16(20480, 16384).rearrange("p (k t) -> p k t", k=8)
            ring = WRing([36864, 36864 + MT], MT)
            SG = [A32(45568 + i * 1024, 512) for i in range(2)]
            MA = [A32(47616 + i * 1024, 512) for i in range(2)]
            MU = [A32(49664 + i * 1024, 512) for i in range(2)]
            M2 = [A32(51712 + i * 1024, 512) for i in range(2)]
            srcs = ((OAT, 4, 0), (OBT, 2, 4), (OCT, 4, 6))
            n = 0
            for fo in range(8):
                wt, rw = ring.load(OFF_M + fo * MT, MT)
                for tc in range(4):
                    tsl = slice(tc * 512, (tc + 1) * 512)
                    b = n % 2
                    n += 1
                    for j, (src, nec, eoff) in enumerate(srcs):
                        ky, kg = bank.next(), bank.next()
                        for ec in range(nec):
                            c0 = 3072 + (eoff + ec) * 128
                            MM(ps[ky][:, :], wt[:, c0:c0 + 128], src[:, ec, tsl], ec == 0, ec == nec - 1, r=[rw], w=[PS(ky)])
                        for kc in range(8):
                            c0 = kc * 384 + j * 128
                            MM(ps[kg][:, :], wt[:, c0:c0 + 128], hT[:, kc, tsl], kc == 0, kc == 7, r=[rw], w=[PS(kg)])
                        bcol = cf[:, C_BG + j * 8 + fo: C_BG + j * 8 + fo + 1]
                        sb = (3 * n + j) % 2
                        ACT(SG[sb], ps[kg][:, :], AF.Sigmoid, r=[PS(kg)], w=[("sg", sb)], bias=bcol)
                        if j == 0:
                            TT("dve", MA[b], SG[sb], ps[ky][:, :], ALU.mult, r=[("sg", sb), PS(ky)], w=[("ma", b)])
                        elif j == 1:
                            TT("dve", MU[b], SG[sb], ps[ky][:, :], ALU.mult, r=[("sg", sb), PS(ky)], w=[("mu", b)])
                            TT("pool", M2[b], MA[b], MU[b], ALU.add, r=[("ma", b), ("mu", b)], w=[("m2", b)])
                        else:
                            TT("dve", MU[b], SG[sb], ps[ky][:, :], ALU.mult, r=[("sg", sb), PS(ky)], w=[("mu", b)])
                            TT("pool", MG[:, fo, tsl], M2[b], MU[b], ALU.add, r=[("m2", b), ("mu", b)], w=[("mg", fo, tc)])
            S.barrier()
            if dbg and s == 0:
                DMA(dbg_d["mg"], A16(20480, 16384), r=[], w=[], name="dbg")

            S.phase = f"s{s}_M2"
            WO = A16(0, 8192).rearrange("p (a c) -> p a c", a=2)
            DMA(WO[:, 0, :], wbf_d[:, OFF_WO:OFF_WO + T8], r=wbf_res(OFF_WO, T8), w=["wo0"], name="w0")
            DMA(WO[:, 1, :], wbf_d[:, OFF_WO + T8:OFF_WO + 2 * T8], r=wbf_res(OFF_WO + T8, T8), w=["wo1"], name="w1")
            XT = [A32(8192 + i * 2048, 1024) for i in range(2)]
            TTB = [A32(12288 + i * 2048, 1024) for i in range(2)]
            X1 = [A32(16384 + i * 2048, 1024) for i in range(2)]
            HN = [A16(36864 + i * 1024, 1024) for i in range(2)]
            junk = A16(38912, 1024)
            sm = A32(39936, 128)
            gT = cf[:, C_GTPF:C_GTPF + 8].unsqueeze(2).broadcast_to([128, 8, 128])
            m2st = {}

            def m2_a(tb):
                b = tb % 2
                o = (tb % 4) * 16
                rows = slice(s * SEQ + tb * 128, s * SEQ + (tb + 1) * 128)
                DMA(XT[b], x_d[rows, :], r=[], w=[("xt", b)], name=f"x{b}")
                kh = [bank.next(), bank.next()]
                m2st[tb] = kh
                S.add("act", lambda e, a_=sm[:, o:o + 2]: e.memzero(a_), w=[("ssq", tb % 4, 0), ("ssq", tb % 4, 1)])
                S.add("act", lambda e, a_=sm[:, o + 8:o + 9]: e.memzero(a_), w=[("ssq3", tb % 4)])
                for half in range(2):
                    for fo in range(8):
                        MM(ps[kh[half]][:, :], MG[:, fo, tb * 128:(tb + 1) * 128], WO[:, half, fo * 512:(fo + 1) * 512], fo == 0, fo == 7,
                           r=["wo0", "wo1"], w=[PS(kh[half])])

            def m2_a2(tb):
                o = (tb % 4) * 16
                kh = m2st[tb]
                for half in range(2):
                    ACT(junk[:, 0:512], ps[kh[half]][:, :], AF.Square, r=[PS(kh[half])], w=["junk", ("ssq", tb % 4, half)],
                        accum_out=sm[:, o + half:o + half + 1])

            def m2_b1(tb):
                b = tb % 2
                o = (tb % 4) * 16
                kh = m2st[tb]
                rows = slice(s * SEQ + tb * 128, s * SEQ + (tb + 1) * 128)
                TT("dve", sm[:, o + 2:o + 3], sm[:, o:o + 1], sm[:, o + 1:o + 2], ALU.add, r=[("ssq", tb % 4, 0), ("ssq", tb % 4, 1)], w=[("ssqt", tb % 4)])
                rstd_from_ssq(sm[:, o + 2:o + 3], sm[:, o + 3:o + 4], sm[:, o + 4:o + 5], r=[("ssqt", tb % 4)], w=("rstd", tb % 4))
                for half in range(2):
                    hs = slice(half * 512, (half + 1) * 512)
                    STT("dve", TTB[b][:, hs], ps[kh[half]][:, :], sm[:, o + 4:o + 5], gpm[:, hs], ALU.mult, ALU.mult,
                        r=[PS(kh[half]), ("rstd", tb % 4)], w=[("ttb", b, half)])
                TT("dve", X1[b], TTB[b], XT[b], ALU.add, r=[("ttb", b, 0), ("ttb", b, 1), ("xt", b)], w=[("x1", b)])
                DMA(out_d[rows, :], X1[b], r=[("x1", b)], w=[], name=f"o{b}", q="pool")

            def m2_b2(tb):
                b = tb % 2
                o = (tb % 4) * 16
                ACT(junk, X1[b], AF.Square, r=[("x1", b)], w=["junk", ("ssq3", tb % 4)], accum_out=sm[:, o + 8:o + 9])
                rstd_from_ssq(sm[:, o + 8:o + 9], sm[:, o + 9:o + 10], sm[:, o + 10:o + 11], r=[("ssq3", tb % 4)], w=("rstd3", tb % 4))
                ACT(HN[b], X1[b], AF.Copy, r=[("x1", b), ("rstd3", tb % 4)], w=[("hn", b)], scale=sm[:, o + 10:o + 11])

            def m2_c(tb):
                b = tb % 2
                k = bank.next()
                psT = ps[k][:, :].bitcast(BF16)
                for kc in range(8):
                    TR(psT[:, kc * 128:(kc + 1) * 128], HN[b][:, kc * 128:(kc + 1) * 128], r=[("hn", b)], w=[PS(k)])
                TT("dve", hT[:, :, tb * 128:(tb + 1) * 128], psT.rearrange("p (k t) -> p k t", k=8), gT, ALU.mult, r=[PS(k)], w=[("h2T", tb)])

            for t in range(NTB + 3):
                if t < NTB:
                    m2_a(t)
                if 0 <= t - 3 < NTB:
                    m2_c(t - 3)
                if 0 <= t - 1 < NTB:
                    m2_b1(t - 1)
                if 0 <= t - 2 < NTB:
                    m2_b2(t - 2)
                if t < NTB:
                    m2_a2(t)
            S.barrier()

            S.phase = f"s{s}_F"
            WFO = A16(0, NJ * 1024).rearrange("p (k c) -> p k c", k=NJ)
            half_n = 11 * 1024
            DMA(A16(0, half_n), wbf_d[:, OFF_FO:OFF_FO + half_n], r=wbf_res(OFF_FO, half_n), w=["wfo0"], name="w3")
            DMA(A16(half_n, half_n), wbf_d[:, OFF_FO + half_n:OFF_FO + 2 * half_n], r=wbf_res(OFF_FO + half_n, half_n), w=["wfo1"], name="w4")
            FT = A16(22528, NJ * 512).rearrange("p (k t) -> p k t", k=NJ)
            ring = WRing([33792, 37888, 41984, 61440], T8)
            SL = [A32(46080 + i * 1024, 512) for i in range(2)]
            TTB = [A32(48128 + i * 2048, 1024) for i in range(2)]
            X1 = [A32(52224 + i * 2048, 1024) for i in range(2)]
            OT = [A32(56320 + i * 2048, 1024) for i in range(2)]
            junk = A16(60416, 512)
            sm = A32(60928, 128)
            n = 0
            for tc in range(4):
                tsl = slice(tc * 512, (tc + 1) * 512)
                for t in range(11):
                    wt, rw = ring.load(OFF_FI + t * T8, T8)
                    for jj in range(2):
                        j = 2 * t + jj
                        kg, ku = bank.next(), bank.next()
                        for kc in range(8):
                            c0 = kc * 512 + jj * 128
                            MM(ps[kg][:, :], wt[:, c0:c0 + 128], hT[:, kc, tsl], kc == 0, kc == 7, r=[rw], w=[PS(kg)])
                        for kc in range(8):
                            c0 = kc * 512 + 256 + jj * 128
                            MM(ps[ku][:, :], wt[:, c0:c0 + 128], hT[:, kc, tsl], kc == 0, kc == 7, r=[rw], w=[PS(ku)])
                        b = n % 2
                        n += 1
                        ACT(SL[b], ps[kg][:, :], AF.Silu, r=[PS(kg)], w=[("sl", b)])
                        TT("dve", FT[:, j, :], SL[b], ps[ku][:, :], ALU.mult, r=[("sl", b), PS(ku)], w=[("ft", j)])
                ft_all = [("ft", j) for j in range(NJ)]
                for tb4 in range(4):
                    tb = tc * 4 + tb4
                    b = tb % 2
                    rows = slice(s * SEQ + tb * 128, s * SEQ + (tb + 1) * 128)
                    DMA(X1[b], out_d[rows, :], r=[], w=[("x1", b)], name=f"x{b}")
                    kh = [bank.next(), bank.next()]
                    S.add("act", lambda e, a_=sm[:, 0:2]: e.memzero(a_), w=[("ssq", 0), ("ssq", 1)])
                    for half in range(2):
                        for j in range(NJ):
                            MM(ps[kh[half]][:, :], FT[:, j, tb4 * 128:(tb4 + 1) * 128], WFO[:, j, half * 512:(half + 1) * 512], j == 0, j == NJ - 1,
                               r=ft_all + ["wfo0", "wfo1"], w=[PS(kh[half])])
                        ACT(junk, ps[kh[half]][:, :], AF.Square, r=[PS(kh[half])], w=["junk", ("ssq", half)], accum_out=sm[:, half:half + 1])
                    TT("dve", sm[:, 2:3], sm[:, 0:1], sm[:, 1:2], ALU.add, r=[("ssq", 0), ("ssq", 1)], w=["ssqt"])
                    rstd_from_ssq(sm[:, 2:3], sm[:, 3:4], sm[:, 4:5], r=["ssqt"], w=("rstd",))
                    for half in range(2):
                        hs = slice(half * 512, (half + 1) * 512)
                        STT("dve", TTB[b][:, hs], ps[kh[half]][:, :], sm[:, 4:5], gpf[:, hs], ALU.mult, ALU.mult,
                            r=[PS(kh[half]), ("rstd",)], w=[("ttb", b, half)])
                    TT("dve", OT[b], TTB[b], X1[b], ALU.add, r=[("ttb", b, 0), ("ttb", b, 1), ("x1", b)], w=[("ot", b)])
                    DMA(out_d[rows, :], OT[b], r=[("ot", b)], w=[], name=f"o{b}", q="pool")
            S.barrier()

        S.finalize()
        semtab = {}
        for en in S.ENGS:
            for ep in range(S.nepoch[en]):
                semtab[("eng", en, ep)] = es.enter_context(nc.semaphore(f"s_{en}_{ep}"))
        for name in S.dma_cnt:
            semtab[("dma", name)] = es.enter_context(nc.semaphore(f"d_{name}"))
        block = es.enter_context(nc.Block())

        @block.tensor
        def _(e):
            S.emit("pe", e, semtab, nc if scopes else None)

        @block.scalar
        def _(e):
            S.emit("act", e, semtab, nc if scopes else None)

        @block.vector
        def _(e):
            S.emit("dve", e, semtab, nc if scopes else None)

        @block.gpsimd
        def _(e):
            S.emit("pool", e, semtab, nc if scopes else None)

        @block.sync
        def _(e):
            S.emit("sp", e, semtab, nc if scopes else None)
    return nc, len(S.ops)


def _t8(w):
    return w.reshape(8, 128, 512).transpose(1, 0, 2).reshape(128, 4096)


def pack_weights(inp):
    w_in = inp["w_in"][0]
    tiles = []
    tiles += [_t8(w_in[:, 0:512]), _t8(w_in[:, 512:1024]), _t8(w_in[:, 1024:1536])]
    swap = np.concatenate([np.arange(h * 64 + 32, h * 64 + 64).tolist() + np.arange(h * 64, h * 64 + 32).tolist() for h in range(4)]).astype(np.int64)
    vcols = []
    for g in range(3):
        base = 1536 + g * 768
        q = w_in[:, base:base + 256]
        k = w_in[:, base + 256:base + 512]
        vcols.append(w_in[:, base + 512:base + 768])
        tiles.append(_t8(np.concatenate([q, q[:, swap]], axis=1)))
        tiles.append(_t8(np.concatenate([k, k[:, swap]], axis=1)))
    tiles.append(_t8(np.concatenate([vcols[0], vcols[1]], axis=1)))
    tiles.append(_t8(np.concatenate([vcols[2], np.zeros((1024, 256), np.float32)], axis=1)))
    tiles.append(_t8(w_in[:, 3840:4352]))
    wm = inp["w_mem_kv"][0]
    tiles += [_t8(wm[:, 0:512]), _t8(wm[:, 512:1024])]
    wg = inp["w_gate"][0]
    wbr = np.concatenate([inp["w_br_sb"][0], inp["w_br_dil"][0], inp["w_br_mem"][0]], axis=0)
    for fo in range(8):
        gsel = np.concatenate([wg[:, j * 1024 + fo * 128: j * 1024 + fo * 128 + 128] for j in range(3)], axis=1)
        gpart = gsel.reshape(8, 128, 384).transpose(1, 0, 2).reshape(128, 3072)
        bpart = wbr[:, fo * 128:(fo + 1) * 128].reshape(10, 128, 128).transpose(1, 0, 2).reshape(128, 1280)
        tiles.append(np.concatenate([gpart, bpart], axis=1))
    wo = inp["w_o"][0]
    tiles += [_t8(wo[:, 0:512]), _t8(wo[:, 512:1024])]
    wfi = inp["w_ffn_in"][0]
    for t in range(11):
        tiles.append(_t8(np.concatenate([wfi[:, t * 256:(t + 1) * 256], wfi[:, D_FF + t * 256: D_FF + (t + 1) * 256]], axis=1)))
    wfo = inp["w_ffn_out"][0]
    tiles.append(wfo.reshape(NJ, 128, 1024).transpose(1, 0, 2).reshape(128, NJ * 1024))
    wall = np.ascontiguousarray(np.concatenate(tiles, axis=1), dtype=np.float32)
    assert wall.shape == (128, TOT), wall.shape
    return wall


def pack_consts(inp):
    cfa = np.zeros((128, NCF + NCB), np.float32)
    inv_freq = (np.float32(10000.0) ** (-np.arange(32, dtype=np.float32) * np.float32(2.0) / np.float32(64))).astype(np.float32)
    ang = np.arange(SEQ, dtype=np.float32)[None, :] * inv_freq[:, None]
    cos = np.cos(ang).astype(np.float32)
    sin = np.sin(ang).astype(np.float32)
    p = np.arange(128)
    fi = (p % 64) % 32
    sign = np.where((p % 64) < 32, -1.0, 1.0).astype(np.float32)
    cfa[:, C_COS:C_COS + SEQ] = cos[fi]
    cfa[:, C_SIN:C_SIN + SEQ] = sin[fi] * sign[:, None]
    cfa[:, C_GPM:C_GPM + D] = np.broadcast_to(inp["g_post_mix"][0][None, :], (128, D))
    cfa[:, C_GPF:C_GPF + D] = np.broadcast_to(inp["g_post_ffn"][0][None, :], (128, D))
    cfa[:, C_GTPM:C_GTPM + 8] = inp["g_pre_mix"][0].reshape(8, 128).T
    cfa[:, C_GTPF:C_GTPF + 8] = inp["g_pre_ffn"][0].reshape(8, 128).T
    cfa[:, C_GTMEM:C_GTMEM + 8] = inp["g_mem"][0].reshape(8, 128).T
    cfa[:, C_BG:C_BG + 24] = inp["b_gate"][0].reshape(24, 128).T
    o = NCF
    kk = np.arange(128)[:, None]
    qq = np.arange(128)[None, :]
    cfa[:, o:o + 128] = np.eye(128, dtype=np.float32)
    cfa[:, o + 128:o + 256] = np.where(kk >= qq, -30000.0, 0.0)
    cfa[:, o + 256:o + 384] = np.where(kk >= qq, -1.0, 0.0)
    mprev = np.where(kk < qq, -30000.0, 0.0)
    mcur = np.where(kk > qq, -30000.0, 0.0)
    cfa[:, o + 384:o + 896] = np.concatenate([mprev, mprev, mcur, mcur], axis=1)
    sw = np.arange(128)
    sw = (sw // 64) * 64 + ((sw % 64) + 32) % 64
    pm = np.zeros((128, 128), np.float32)
    pm[sw, np.arange(128)] = 1.0
    cfa[:, o + 896:o + 1024] = pm
    cfa[:, o + 1024:o + 1536] = (cfa[:, o + 384:o + 896] == 0.0).astype(np.float32)
    return cfa


_CACHE = {}


def _get_program(nseq, dbg=False):
    key = (nseq, dbg)
    if key not in _CACHE:
        _CACHE[key] = build_program(nseq, dbg)[0]
    return _CACHE[key]


def kernel(**inputs):
    inp = {k: np.asarray(v) for k, v in inputs.items()}
    x = np.ascontiguousarray(inp["x"], dtype=np.float32)
    mem = np.ascontiguousarray(inp["mem"], dtype=np.float32)
    B = x.shape[0]
    nseq = B // NCORES
    wall = pack_weights(inp)
    cfa = pack_consts(inp)
    nc = _get_program(nseq)
    in_maps = []
    for c in range(NCORES):
        in_maps.append({
            "x": x[c * nseq:(c + 1) * nseq].reshape(nseq * SEQ, D),
            "mem": mem[c * nseq:(c + 1) * nseq].reshape(nseq * MEM, D),
            "wall": wall,
            "cf": cfa,
        })
    res = run_bass_kernel_spmd(nc, in_maps, core_ids=list(range(NCORES)))
    outs = [np.asarray(r["out"]).reshape(nseq, SEQ, D) for r in res.results]
    return np.concatenate(outs, axis=0).astype(np.float32)
```
